# Optimizing a Trainium2 kernel written in Bass

```python
import math
import jax, jax.numpy as jnp
from jax import lax
import numpy as np

D_MODEL = 1024
BATCH = 16
SEQ = 2048
DEPTH = 2

EPS = 1e-6
PLE_DIM = 256
D_FF = 2816
CONV_W = 4
HEAD_DIM = 64
LRU_WIDTH = D_MODEL // 4
LRU_BLOCKS = LRU_WIDTH // HEAD_DIM
LRU_BLOCK = LRU_WIDTH // LRU_BLOCKS
LRU_C = 8.0
ATT_WIDTH = D_MODEL // 2
ATT_HEADS = ATT_WIDTH // HEAD_DIM
ATT_KV_HEADS = 2
ATT_GROUP = ATT_HEADS // ATT_KV_HEADS
KV_WIDTH = ATT_KV_HEADS * HEAD_DIM
WINDOW = 128
BLOCK_Q = 128
REL_BUCKETS = 32
REL_MAX_DIST = 128
DN_WIDTH = D_MODEL // 4
DN_HEADS = DN_WIDTH // HEAD_DIM
DN_DK = HEAD_DIM
DN_DV = HEAD_DIM
DN_QK = DN_HEADS * DN_DK
DN_CHUNK = 64
D_MIX = LRU_WIDTH + ATT_WIDTH + DN_WIDTH
IN_SPLITS = (LRU_WIDTH, LRU_WIDTH,
             ATT_WIDTH, KV_WIDTH, KV_WIDTH,
             DN_QK, DN_QK, DN_WIDTH, DN_WIDTH,
             DN_HEADS, DN_HEADS)
D_IN = sum(IN_SPLITS)

kernel_name = "hymba_style_lru_swa_deltanet_macaron"


def rms_norm(x, g):
    xf = x.astype(jnp.float32)
    y = xf * lax.rsqrt(jnp.mean(xf * xf, axis=-1, keepdims=True) + EPS)
    return (y * g.astype(jnp.float32)).astype(x.dtype)


def swiglu(x, w_gate, w_up, w_down):
    return (jax.nn.silu(x @ w_gate) * (x @ w_up)) @ w_down


def causal_dwconv(x, w, b=None):
    K = w.shape[0]
    S = x.shape[1]
    xp = jnp.pad(x, ((0, 0), (K - 1, 0), (0, 0)))
    y = xp[:, 0:S] * w[0]
    for k in range(1, K):
        y = y + xp[:, k:k + S] * w[k]
    if b is not None:
        y = y + b
    return y


def split_points():
    return np.cumsum(np.array(IN_SPLITS))[:-1].tolist()


def rg_lru(x, w_a, b_a, w_x, b_x, lam):
    B, S, _ = x.shape
    xb = x.reshape(B, S, LRU_BLOCKS, LRU_BLOCK)
    r = jax.nn.sigmoid(jnp.einsum('bshi,hij->bshj', xb, w_a).reshape(B, S, LRU_WIDTH) + b_a)
    i = jax.nn.sigmoid(jnp.einsum('bshi,hij->bshj', xb, w_x).reshape(B, S, LRU_WIDTH) + b_x)
    log_a = -LRU_C * r.astype(jnp.float32) * jax.nn.softplus(-lam.astype(jnp.float32))
    a = jnp.exp(log_a)
    u = jnp.sqrt(-jnp.expm1(2.0 * log_a)) * (i * x).astype(jnp.float32)

    def combine(left, right):
        a1, b1 = left
        a2, b2 = right
        return a1 * a2, a2 * b1 + b2

    _, h = lax.associative_scan(combine, (a, u), axis=1)
    return h.astype(x.dtype)


def rel_bucket(dist):
    max_exact = REL_BUCKETS // 2
    large = max_exact + (jnp.log(jnp.maximum(dist, 1).astype(jnp.float32) / max_exact)
                         / math.log(REL_MAX_DIST / max_exact)
                         * (REL_BUCKETS - max_exact)).astype(jnp.int32)
    large = jnp.minimum(large, REL_BUCKETS - 1)
    return jnp.where(dist < max_exact, dist, large)


def swa_attention(q, k, v, sinks, rel_bias):
    B, S = q.shape[:2]
    NB = S // BLOCK_Q
    qb = q.reshape(B, NB, BLOCK_Q, ATT_KV_HEADS, ATT_GROUP, HEAD_DIM)

    def with_prev(t):
        tb = t.reshape(B, NB, BLOCK_Q, ATT_KV_HEADS, HEAD_DIM)
        prev = jnp.pad(tb, ((0, 0), (1, 0), (0, 0), (0, 0), (0, 0)))[:, :-1]
        return jnp.concatenate([prev, tb], axis=2)

    kb, vb = with_prev(k), with_prev(v)
    qi = jnp.arange(BLOCK_Q)[:, None]
    kj = jnp.arange(2 * BLOCK_Q)[None, :]
    dist = BLOCK_Q + qi - kj
    band = (dist >= 0) & (dist < WINDOW)
    blk = jnp.arange(NB)[:, None, None]
    valid = band[None] & ((blk > 0) | (kj[None] >= BLOCK_Q))
    bias = rel_bias.astype(jnp.float32)[rel_bucket(jnp.maximum(dist, 0))]
    bias = bias.transpose(2, 0, 1).reshape(ATT_KV_HEADS, ATT_GROUP, BLOCK_Q, 2 * BLOCK_Q)
    s = jnp.einsum('bnikgd,bnjkd->bnkgij', qb, kb).astype(jnp.float32) * (HEAD_DIM ** -0.5) + bias
    s = jnp.where(valid[None, :, None, None], s, -jnp.inf)
    sink = sinks.astype(jnp.float32).reshape(ATT_KV_HEADS, ATT_GROUP)[:, :, None, None]
    m = jnp.maximum(jnp.max(s, axis=-1, keepdims=True), sink)
    e = jnp.exp(s - m)
    probs = e / (jnp.sum(e, axis=-1, keepdims=True) + jnp.exp(sink - m))
    o = jnp.einsum('bnkgij,bnjkd->bnikgd', probs.astype(v.dtype), vb)
    return o.reshape(B, S, ATT_WIDTH)


def l2norm(t):
    return t * lax.rsqrt(jnp.sum(t * t, axis=-1, keepdims=True) + EPS)


def gated_delta_rule(q, k, v, g, beta):
    B, S, H, DK = k.shape
    DV = v.shape[-1]
    C = DN_CHUNK
    NC = S // C
    f32 = jnp.float32
    q = l2norm(q.astype(f32)) * (DK ** -0.5)
    k = l2norm(k.astype(f32))

    def chunks(t):
        return t.reshape(B, NC, C, H, -1).transpose(1, 0, 3, 2, 4)

    qc, kc, vc = chunks(q), chunks(k), chunks(v.astype(f32))
    gc = g.astype(f32).reshape(B, NC, C, H).transpose(1, 0, 3, 2)
    bc = beta.astype(f32).reshape(B, NC, C, H).transpose(1, 0, 3, 2)
    gcum = jnp.cumsum(gc, axis=-1)
    tril = jnp.tril(jnp.ones((C, C), dtype=bool))
    strict = jnp.tril(jnp.ones((C, C), dtype=bool), -1)
    decay = jnp.exp(jnp.where(tril, gcum[..., :, None] - gcum[..., None, :], -jnp.inf))
    k_beta = kc * bc[..., None]
    v_beta = vc * bc[..., None]
    Lmat = jnp.where(strict, jnp.einsum('...id,...jd->...ij', k_beta, kc) * decay, 0.0)
    eye = jnp.eye(C, dtype=f32)
    T = lax.linalg.triangular_solve(Lmat + eye, jnp.broadcast_to(eye, Lmat.shape),
                                    left_side=True, lower=True, unit_diagonal=True)
    u = jnp.einsum('...ij,...jd->...id', T, v_beta)
    w = jnp.einsum('...ij,...jd->...id', T, k_beta * jnp.exp(gcum)[..., None])

    def step(state, xs):
        q_i, k_i, u_i, w_i, g_i, dec_i = xs
        attn = jnp.einsum('bhid,bhjd->bhij', q_i, k_i) * dec_i
        v_new = u_i - jnp.einsum('bhcd,bhde->bhce', w_i, state)
        o = (jnp.einsum('bhcd,bhde->bhce', q_i * jnp.exp(g_i)[..., None], state)
             + jnp.einsum('bhij,bhje->bhie', attn, v_new))
        g_last = g_i[..., -1]
        k_dec = k_i * jnp.exp(g_last[..., None] - g_i)[..., None]
        state = state * jnp.exp(g_last)[..., None, None] + jnp.einsum('bhcd,bhce->bhde', k_dec, v_new)
        return state, o

    state0 = jnp.zeros((B, H, DK, DV), f32)
    _, o = lax.scan(step, state0, (qc, kc, u, w, gcum, decay))
    return o.transpose(1, 0, 3, 2, 4).reshape(B, S, H, DV)


def hybrid_mixer(xn, w_in, lru_conv_w, lru_conv_b, lru_w_a, lru_b_a, lru_w_x, lru_b_x, lru_lambda,
                 attn_sinks, rel_bias, dn_conv_w, dn_a_log, dn_dt_bias, dn_norm, w_out):
    B, S, _ = xn.shape
    u = xn @ w_in
    (lru_x, lru_gate, att_q, att_k, att_v, dn_q, dn_k, dn_v, dn_z, dn_b, dn_a) = jnp.split(
        u, split_points(), axis=-1)
    xr = causal_dwconv(lru_x, lru_conv_w, lru_conv_b)
    y_lru = jax.nn.gelu(lru_gate) * rg_lru(xr, lru_w_a, lru_b_a, lru_w_x, lru_b_x, lru_lambda)
    y_att = swa_attention(att_q.reshape(B, S, ATT_HEADS, HEAD_DIM),
                          att_k.reshape(B, S, ATT_KV_HEADS, HEAD_DIM),
                          att_v.reshape(B, S, ATT_KV_HEADS, HEAD_DIM),
                          attn_sinks, rel_bias)
    qkv = jax.nn.silu(causal_dwconv(jnp.concatenate([dn_q, dn_k, dn_v], axis=-1), dn_conv_w))
    q, k, v = jnp.split(qkv, [DN_QK, 2 * DN_QK], axis=-1)
    beta = jax.nn.sigmoid(dn_b.astype(jnp.float32))
    g = -jnp.exp(dn_a_log.astype(jnp.float32)) * jax.nn.softplus(
        dn_a.astype(jnp.float32) + dn_dt_bias.astype(jnp.float32))
    o = gated_delta_rule(q.reshape(B, S, DN_HEADS, DN_DK), k.reshape(B, S, DN_HEADS, DN_DK),
                         v.reshape(B, S, DN_HEADS, DN_DV), g, beta)
    z = dn_z.reshape(B, S, DN_HEADS, DN_DV).astype(jnp.float32)
    o = (o * lax.rsqrt(jnp.mean(o * o, axis=-1, keepdims=True) + EPS)
         * dn_norm.astype(jnp.float32) * jax.nn.silu(z))
    y_dn = o.reshape(B, S, DN_WIDTH).astype(xn.dtype)
    return jnp.concatenate([y_lru, y_att, y_dn], axis=-1) @ w_out


def setup_inputs(seed: int = 0) -> dict:
    key = jax.random.key(seed)
    ks = list(jax.random.split(key, 48))

    def nrm(shape, scale):
        return scale * jax.random.normal(ks.pop(), shape, jnp.float32)

    def gain(shape):
        return 1.0 + 0.02 * jax.random.normal(ks.pop(), shape, jnp.float32)

    L, D = DEPTH, D_MODEL
    x = nrm((BATCH, SEQ, D), 1.0)
    p = nrm((DEPTH, BATCH, SEQ, PLE_DIM), 1.0)
    a_c = jax.random.uniform(ks.pop(), (L, LRU_WIDTH), jnp.float32, 0.9, 0.999)
    s = a_c ** (1.0 / LRU_C)
    lru_lambda = jnp.log(s) - jnp.log1p(-s)
    dn_a_log = jnp.log(jax.random.uniform(ks.pop(), (L, DN_HEADS), jnp.float32, 1.0, 16.0))
    dt = jnp.exp(jax.random.uniform(ks.pop(), (L, DN_HEADS), jnp.float32,
                                    math.log(1e-3), math.log(1e-1)))
    dn_dt_bias = dt + jnp.log(-jnp.expm1(-dt))
    return {
        "x": x,
        "p": p,
        "ffn1_norm": gain((L, D)),
        "ffn1_w_gate": nrm((L, D, D_FF), D ** -0.5),
        "ffn1_w_up": nrm((L, D, D_FF), D ** -0.5),
        "ffn1_w_down": nrm((L, D_FF, D), D_FF ** -0.5),
        "mix_norm": gain((L, D)),
        "w_in": nrm((L, D, D_IN), D ** -0.5),
        "lru_conv_w": nrm((L, CONV_W, LRU_WIDTH), CONV_W ** -0.5),
        "lru_conv_b": nrm((L, LRU_WIDTH), 0.01),
        "lru_w_a": nrm((L, LRU_BLOCKS, LRU_BLOCK, LRU_BLOCK), LRU_BLOCK ** -0.5),
        "lru_b_a": nrm((L, LRU_WIDTH), 0.01),
        "lru_w_x": nrm((L, LRU_BLOCKS, LRU_BLOCK, LRU_BLOCK), LRU_BLOCK ** -0.5),
        "lru_b_x": nrm((L, LRU_WIDTH), 0.01),
        "lru_lambda": lru_lambda,
        "attn_sinks": nrm((L, ATT_HEADS), 0.5),
        "rel_bias": nrm((REL_BUCKETS, ATT_HEADS), 0.5),
        "dn_conv_w": nrm((L, CONV_W, 2 * DN_QK + DN_WIDTH), CONV_W ** -0.5),
        "dn_a_log": dn_a_log,
        "dn_dt_bias": dn_dt_bias,
        "dn_norm": gain((L, DN_DV)),
        "w_out": nrm((L, D_MIX, D), D_MIX ** -0.5),
        "ffn2_norm": gain((L, D)),
        "ffn2_w_gate": nrm((L, D, D_FF), D ** -0.5),
        "ffn2_w_up": nrm((L, D, D_FF), D ** -0.5),
        "ffn2_w_down": nrm((L, D_FF, D), D_FF ** -0.5),
        "ple_norm": gain((L, D)),
        "ple_w_gate": nrm((L, D, D), D ** -0.5),
        "ple_w_proj": nrm((L, PLE_DIM, D), PLE_DIM ** -0.5),
        "final_norm": gain((D,)),
    }


def reference(x, p, ffn1_norm, ffn1_w_gate, ffn1_w_up, ffn1_w_down, mix_norm, w_in,
              lru_conv_w, lru_conv_b, lru_w_a, lru_b_a, lru_w_x, lru_b_x, lru_lambda,
              attn_sinks, rel_bias, dn_conv_w, dn_a_log, dn_dt_bias, dn_norm, w_out,
              ffn2_norm, ffn2_w_gate, ffn2_w_up, ffn2_w_down, ple_norm, ple_w_gate, ple_w_proj,
              final_norm):
    h = x
    for l in range(DEPTH):
        h = h + 0.5 * swiglu(rms_norm(h, ffn1_norm[l]), ffn1_w_gate[l], ffn1_w_up[l], ffn1_w_down[l])
        h = h + hybrid_mixer(rms_norm(h, mix_norm[l]), w_in[l],
                             lru_conv_w[l], lru_conv_b[l], lru_w_a[l], lru_b_a[l],
                             lru_w_x[l], lru_b_x[l], lru_lambda[l],
                             attn_sinks[l], rel_bias,
                             dn_conv_w[l], dn_a_log[l], dn_dt_bias[l], dn_norm[l], w_out[l])
        h = h + 0.5 * swiglu(rms_norm(h, ffn2_norm[l]), ffn2_w_gate[l], ffn2_w_up[l], ffn2_w_down[l])
        gate = jax.nn.sigmoid(rms_norm(h, ple_norm[l]) @ ple_w_gate[l])
        h = h + gate * (p[l] @ ple_w_proj[l])
    return rms_norm(h, final_norm)
```

```python
import math
import numpy as np
import concourse.bass as bass
import concourse.mybir as mybir
from concourse.bass_utils import run_bass_kernel_spmd

F32 = mybir.dt.float32
BF16 = mybir.dt.bfloat16
AF = mybir.ActivationFunctionType
ALU = mybir.AluOpType
AX = mybir.AxisListType

COMPUTE = ("pe", "act", "dve", "pool")
SEM_WRAP = 30000
DMA_RING = 8

D = 1024
SEQ = 2048
DEPTH = 2
DFF = 2816
NFC = DFF // 128
PLE = 256
EPS = 1e-6
NEG = -30000.0


class Op:
    __slots__ = ("eng", "fn", "reads", "writes", "dma", "idx", "waits", "signal",
                 "sem", "semval", "prewait", "tag")

    def __init__(self, eng, fn, reads, writes, dma, tag=""):
        self.eng = eng
        self.fn = fn
        self.reads = reads
        self.writes = writes
        self.dma = dma
        self.waits = []
        self.signal = False
        self.sem = None
        self.semval = 0
        self.prewait = None
        self.tag = tag


class Sched:
    def __init__(self, nc, same_engine_sync=True):
        self.nc = nc
        self.ops = []
        self.same_engine_sync = same_engine_sync
        self.ent = {}
        self.children = {}

    def add(self, eng, fn, reads=(), writes=(), dma=False, tag=""):
        rk = [k if isinstance(k, tuple) else (k,) for k in reads]
        wk = [k if isinstance(k, tuple) else (k,) for k in writes]
        op = Op(eng, fn, rk, wk, dma, tag)
        op.idx = len(self.ops)
        self.ops.append(op)
        return op

    def _conflicts(self, k):
        out = []
        for n in range(1, len(k) + 1):
            e = self.ent.get(k[:n])
            if e is not None:
                out.append(e)
        for c in self.children.get(k, ()):
            out.append(self.ent[c])
        return out

    def _entry(self, k):
        e = self.ent.get(k)
        if e is None:
            e = [None, []]
            self.ent[k] = e
            for n in range(1, len(k)):
                self.children.setdefault(k[:n], set()).add(k)
        return e

    def analyze(self):
        ops = self.ops
        waited = {}
        waited_dma = {}
        dma_count = {}
        dma_hist = {}
        for op in ops:
            deps = set()
            for k in op.reads:
                for e in self._conflicts(k):
                    if e[0] is not None:
                        deps.add(e[0])
                    if k[0] == "ps":
                        for r_ in e[1]:
                            if ops[r_].eng != op.eng:
                                deps.add(r_)
            for k in op.writes:
                for e in self._conflicts(k):
                    if e[0] is not None:
                        deps.add(e[0])
                    deps.update(e[1])
            for k in op.reads:
                self._entry(k)[1].append(op.idx)
            for k in op.writes:
                e = self._entry(k)
                e[0] = op.idx
                e[1] = []
                for c in self.children.get(k, ()):
                    ce = self.ent[c]
                    ce[0] = op.idx
                    ce[1] = []
            deps.discard(op.idx)
            w = waited.setdefault(op.eng, {})
            wd = waited_dma.setdefault(op.eng, set())
            best = {}
            for d in deps:
                p = ops[d]
                if p.dma:
                    if d not in wd:
                        wd.add(d)
                        op.waits.append(d)
                    continue
                if p.eng == op.eng and not op.dma:
                    if op.eng == "pe" or not self.same_engine_sync:
                        continue
                if d > best.get(p.eng, -1):
                    best[p.eng] = d
            for e, d in best.items():
                if d > w.get(e, -1):
                    w[e] = d
                    op.waits.append(d)
                    ops[d].signal = True
            if op.dma:
                q = op.eng
                n = dma_count.get(q, 0)
                dma_count[q] = n + 1
                h = dma_hist.setdefault(q, [])
                if n >= DMA_RING:
                    op.prewait = h[n - DMA_RING]
                h.append(op.idx)
                op.signal = True
        return self

    def emit(self, final_wait_ops=()):
        nc = self.nc
        ops = self.ops
        sig_count = {}
        sems = {}
        dma_n = {}
        for op in ops:
            if not op.signal:
                continue
            if op.dma:
                q = op.eng
                n = dma_n.get(q, 0)
                dma_n[q] = n + 1
                ring = sems.setdefault(("dma", q), [])
                if len(ring) < DMA_RING:
                    ring.append(nc.alloc_semaphore(f"d_{q}_{len(ring)}"))
                op.sem = ring[n % DMA_RING]
                op.semval = 16 * (n // DMA_RING + 1)
            else:
                n = sig_count.get(op.eng, 0)
                sig_count[op.eng] = n + 1
                lst = sems.setdefault(op.eng, [])
                if n // SEM_WRAP >= len(lst):
                    lst.append(nc.alloc_semaphore(f"s_{op.eng}_{len(lst)}"))
                op.sem = lst[n // SEM_WRAP]
                op.semval = n % SEM_WRAP + 1
        by_eng = {}
        for op in ops:
            by_eng.setdefault(op.eng, []).append(op)
        self.stats = {e: len(v) for e, v in by_eng.items()}
        self.stats["signals"] = dict(sig_count)
        self.stats["nwaits"] = sum(len(o.waits) for o in ops)

        def run(engh, lst, finals):
            for op in lst:
                if op.prewait is not None:
                    p = ops[op.prewait]
                    engh.wait_ge(p.sem, p.semval)
                for d in op.waits:
                    p = ops[d]
                    engh.wait_ge(p.sem, p.semval)
                ins = op.fn(engh)
                if op.signal:
                    ins.then_inc(op.sem, 16 if op.dma else 1)
            for d in finals:
                p = ops[d]
                engh.wait_ge(p.sem, p.semval)

        finals = [o.idx for o in final_wait_ops]
        with nc.Block() as block:
            @block.sync
            def _(e):
                run(e, by_eng.get("sp", []), finals)

            @block.scalar
            def _(e):
                run(e, by_eng.get("act", []), [])

            @block.vector
            def _(e):
                run(e, by_eng.get("dve", []), [])

            @block.gpsimd
            def _(e):
                run(e, by_eng.get("pool", []), [])

            @block.tensor
            def _(e):
                run(e, by_eng.get("pe", []), [])


class Ring:
    def __init__(self, nc, name, shape, dtype, n):
        self.name = name
        self.bufs = [nc.alloc_sbuf_tensor(f"rg_{name}{i}", list(shape), dtype).ap() for i in range(n)]
        self.gen = [0] * n
        self.n = n
        self.i = 0

    def get(self):
        i = self.i
        self.i = (i + 1) % self.n
        self.gen[i] += 1
        return Tmp(self, i, self.gen[i])


class ViewRing:
    def __init__(self, name, bufs, keys):
        self.name = name
        self.bufs = bufs
        self.keys = keys
        self.n = len(bufs)
        self.gen = [0] * self.n
        self.i = 0

    def get(self):
        i = self.i
        self.i = (i + 1) % self.n
        self.gen[i] += 1
        t = Tmp(self, i, self.gen[i])
        t.key = self.keys[i]
        return t


class Ctx:
    def __init__(self, tmp, tmpb, pad, banks):
        self.tmp = tmp
        self.tmpb = tmpb
        self.pad = pad
        self.banks = banks
        self.i = 0


class Tmp:
    def __init__(self, ring, i, gen):
        self.ring = ring
        self.i = i
        self.g = gen
        self.key = (ring.name, i)

    @property
    def ap(self):
        assert self.ring.gen[self.i] == self.g, f"stale tmp {self.ring.name}[{self.i}]"
        return self.ring.bufs[self.i]


IN_OFF = dict(lru_x=0, lru_g=256, att_q=512, att_k=1024, att_v=1152, dn_q=1280, dn_k=1536,
              dn_v=1792, dn_z=2048, dn_b=2304, dn_a=2308)
NF = 15
NTM = 392


def _fcols():
    cols = []
    for c in range(2):
        cols.append(np.arange(IN_OFF["lru_x"] + c * 128, IN_OFF["lru_x"] + (c + 1) * 128))
    for c in range(2):
        cols.append(np.arange(IN_OFF["lru_g"] + c * 128, IN_OFF["lru_g"] + (c + 1) * 128))
    for g in range(4):
        a = np.arange(IN_OFF["att_q"] + g * 64, IN_OFF["att_q"] + (g + 1) * 64)
        b = np.arange(IN_OFF["att_q"] + (4 + g) * 64, IN_OFF["att_q"] + (5 + g) * 64)
        cols.append(np.concatenate([a, b]))
    cols.append(np.arange(IN_OFF["att_k"], IN_OFF["att_k"] + 128))
    for nm in ("dn_q", "dn_k", "dn_v"):
        for c in range(2):
            cols.append(np.arange(IN_OFF[nm] + c * 128, IN_OFF[nm] + (c + 1) * 128))
    return cols


def _tmcols():
    return np.concatenate([np.arange(IN_OFF["att_v"], IN_OFF["att_v"] + 128),
                           np.arange(IN_OFF["dn_z"], IN_OFF["dn_z"] + 256),
                           np.arange(IN_OFF["dn_b"], IN_OFF["dn_b"] + 4),
                           np.arange(IN_OFF["dn_a"], IN_OFF["dn_a"] + 4)])


def _mix_rows():
    rows = [np.arange(0, 128), np.arange(128, 256)]
    for g in range(4):
        a = np.arange(256 + g * 64, 256 + (g + 1) * 64)
        b = np.arange(256 + (4 + g) * 64, 256 + (5 + g) * 64)
        rows.append(np.concatenate([a, b]))
    rows.append(np.arange(768, 896))
    rows.append(np.arange(896, 1024))
    return np.concatenate(rows)


def _rel_bucket(dist):
    max_exact = 16
    d = np.maximum(dist, 1).astype(np.float32)
    large = max_exact + (np.log(d / np.float32(max_exact)) / np.float32(math.log(128 / max_exact))
                         * np.float32(32 - max_exact)).astype(np.int32)
    large = np.minimum(large, 31)
    return np.where(dist < max_exact, dist, large)


def _kchunk(w):
    return np.ascontiguousarray(w.reshape(8, 128, -1).transpose(1, 0, 2))


def prep_shared(inp):
    f = np.float32
    out = {}
    L = DEPTH
    wgu = np.empty((2, L, NFC, 128, 2, 8, 128), f)
    wd = np.empty((2, L, NFC, 128, 1024), f)
    for fi, nm in enumerate(("ffn1", "ffn2")):
        for l in range(L):
            g = inp[nm + "_w_gate"][l].reshape(8, 128, NFC, 128)
            u = inp[nm + "_w_up"][l].reshape(8, 128, NFC, 128)
            wgu[fi, l, :, :, 0] = g.transpose(2, 1, 0, 3)
            wgu[fi, l, :, :, 1] = u.transpose(2, 1, 0, 3)
            wd[fi, l] = inp[nm + "_w_down"][l].reshape(NFC, 128, 1024)
    out["wgu"] = wgu.reshape(2 * L * NFC, 128, 2048)
    out["wd"] = wd.reshape(2 * L * NFC, 128, 1024)
    fc = _fcols()
    tm = _tmcols()
    winF = np.empty((L, NF, 128, 8, 128), f)
    winT = np.empty((L, 128, 8, NTM), f)
    for l in range(L):
        w = inp["w_in"][l]
        for i, c in enumerate(fc):
            winF[l, i] = _kchunk(w[:, c])
        winT[l] = _kchunk(w[:, tm])
    out["winF"] = winF.reshape(L * NF, 128, 1024)
    out["winT"] = winT.reshape(L, 128, 8 * NTM)
    rows = _mix_rows()
    out["wout"] = np.stack([_kchunk(inp["w_out"][l][rows]) for l in range(L)]).reshape(L, 128, 8 * 1024)
    out["wpg"] = np.stack([_kchunk(inp["ple_w_gate"][l]) for l in range(L)]).reshape(L, 128, 8 * 1024)
    out["wpp"] = np.stack([inp["ple_w_proj"][l].reshape(2, 128, 1024).transpose(1, 0, 2)
                           for l in range(L)]).reshape(L, 128, 2 * 1024).copy()
    pp = np.zeros((L, 128, 72), f)
    for l in range(L):
        for j, nm in enumerate(("ffn1_norm", "mix_norm", "ffn2_norm", "ple_norm")):
            pp[l, :, j * 8:(j + 1) * 8] = inp[nm][l].reshape(8, 128).T
        pp[l, :, 32:40] = inp["lru_conv_w"][l].reshape(4, 2, 128).transpose(2, 1, 0).reshape(128, 8)
        pp[l, :, 40:42] = inp["lru_conv_b"][l].reshape(2, 128).T
        pp[l, :, 42:44] = inp["lru_b_a"][l].reshape(2, 128).T
        pp[l, :, 44:46] = inp["lru_b_x"][l].reshape(2, 128).T
        pp[l, :, 46:48] = inp["lru_lambda"][l].reshape(2, 128).T
        pp[l, :, 48:72] = inp["dn_conv_w"][l].reshape(4, 6, 128).transpose(2, 1, 0).reshape(128, 24)
    out["pp"] = pp
    rp = np.zeros((L, 128, 80), f)
    for l in range(L):
        rp[l, :, 0:8] = inp["attn_sinks"][l][None, :]
        rp[l, :, 8:12] = inp["dn_a_log"][l][None, :]
        rp[l, :, 12:16] = inp["dn_dt_bias"][l][None, :]
        rp[l, :, 16:80] = inp["dn_norm"][l][None, :]
    out["rp"] = rp
    out["fing"] = np.ascontiguousarray(np.broadcast_to(inp["final_norm"][None, :], (128, 1024))).astype(f)
    wab = np.zeros((L, 128, 2, 2, 128), f)
    for l in range(L):
        for ai, nm in enumerate(("lru_w_a", "lru_w_x")):
            for c in range(2):
                for hh in range(2):
                    wab[l, hh * 64:(hh + 1) * 64, ai, c, hh * 64:(hh + 1) * 64] = inp[nm][l, 2 * c + hh]
    out["wab"] = wab.reshape(L, 128, 512)
    rb = inp["rel_bias"]
    j = np.arange(128)[:, None]
    i = np.arange(128)[None, :]
    bias = np.zeros((128, 2, 2, 4, 128), f)
    for kb in range(2):
        dist = 128 + i - (kb * 128 + j)
        valid = (dist >= 0) & (dist < 128)
        bk = _rel_bucket(np.maximum(dist, 0))
        for kh in range(2):
            for g in range(4):
                gathered = rb[bk, kh * 4 + g]
                bias[:, kh, kb, g, :] = np.where(valid, gathered, f(NEG))
    out["abias"] = bias.reshape(128, 2 * 2 * 512)
    cst = np.zeros((128, 640), f)
    cst[:, 0:128] = np.eye(128, dtype=f)
    cst[:, 128:256] = np.triu(np.ones((128, 128), f))
    cst[:, 256:384] = np.tril(np.ones((128, 128), f), -1)
    cst[:, 384:512] = 1.0
    bo = np.zeros((128, 128), f)
    bo[0:64, 0:64] = 1.0
    bo[64:128, 64:128] = 1.0
    cst[:, 512:640] = bo
    out["cst"] = cst
    return out


class Builder:
    def __init__(self, nseq=2, ntiles=4, nlayers=2, T=512, dump=None, tdt=F32, phases=None, mix=None):
        self.nseq = nseq
        self.NT = ntiles
        self.L = nlayers
        self.T = T
        self.NB = T // 128
        self.tdt = tdt
        self.dump = dump or []
        self.blk_stop = 99
        self.ilv = True
        self.ilv_w = (1, 2, 2)
        self.mix_id = 0
        self.phases = phases or ('f1', 'mix', 'f2', 'ple')
        self.mix = mix or ('tm', 'sc', 'lru', 'att', 'pre', 'blk', 'out')
        self.dump_ops = []
        nc = bass.Bass("TRN2", target_bir_lowering=False)
        self.nc = nc
        self.s = Sched(nc)
        self._dram()
        self._sbuf()

    def _dram(self):
        nc = self.nc
        L = DEPTH

        def di(name, shape):
            return nc.dram_tensor(name, list(shape), F32, kind="ExternalInput").ap()
        self.x_d = di("x", (self.nseq, SEQ, D))
        self.p_d = di("p", (L, self.nseq, SEQ, PLE))
        self.wgu_d = di("wgu", (2 * L * NFC, 128, 2048))
        self.wd_d = di("wd", (2 * L * NFC, 128, 1024))
        self.winF_d = di("winF", (L * NF, 128, 1024))
        self.winT_d = di("winT", (L, 128, 8 * NTM))
        self.wout_d = di("wout", (L, 128, 8192))
        self.wpg_d = di("wpg", (L, 128, 8192))
        self.wpp_d = di("wpp", (L, 128, 2048))
        self.pp_d = di("pp", (L, 128, 72))
        self.rp_d = di("rp", (L, 128, 80))
        self.fing_d = di("fing", (128, 1024))
        self.wab_d = di("wab", (L, 128, 512))
        self.abias_d = di("abias", (128, 2048))
        self.cst_d = di("cst", (128, 640))
        self.out_d = nc.dram_tensor("out", [self.nseq, SEQ, D], F32, kind="ExternalOutput").ap()

    def _sb(self, name, shape, dtype):
        return self.nc.alloc_sbuf_tensor("sb_" + name, list(shape), dtype).ap()

    def _sbuf(self):
        nc = self.nc
        NB, T, L = self.NB, self.T, DEPTH
        self.ps = [nc.alloc_psum_tensor(f"ps{i}", [128, 512], F32).ap() for i in range(8)]
        self.cst = self._sb("cst", (128, 640), F32)
        self.ident_f = self.cst[:, 0:128]
        self.U_f = self.cst[:, 128:256]
        self.SL_f = self.cst[:, 256:384]
        self.ones_f = self.cst[:, 384:512]
        self.cstb = self._sb("cstb", (128, 384), BF16)
        self.ident_b = self.cstb[:, 0:128]
        self.ones_b = self.cstb[:, 128:256]
        self.bones_b = self.cstb[:, 256:384]
        self.abias = self._sb("abias", (128, 2, 2, 512), F32)
        self.pp = self._sb("pp", (128, L, 72), F32)
        self.rp = self._sb("rp", (128, L, 80), F32)
        self.fing = self._sb("fing", (128, 1024), F32)
        self.wab = self._sb("wab", (128, L, 2, 2, 128), BF16)
        self.der = self._sb("der", (128, L, 32), F32)
        self.halo_lru = self._sb("halo_lru", (128, L, 2, 3), F32)
        self.halo_dn = self._sb("halo_dn", (128, L, 6, 3), F32)
        self.kTpad = self._sb("kTpad", (128, L, 128 + T), BF16)
        self.vpad = self._sb("vpad", (128, L, NB + 1, 128), BF16)
        self.S = self._sb("S", (128, L, 2, 64), F32)
        self.Sb = self._sb("Sb", (128, L, 2, 64), BF16)
        self.lru_h = self._sb("lru_h", (128, L, 2), F32)
        self.h = self._sb("h", (128, NB, D), F32)
        self.xnT = self._sb("xnT", (128, 8, T), BF16)
        self.yT = self._sb("yT", (128, 8, T), BF16)
        self.GS = 4
        self.hT = self._sb("hT", (128, self.GS, T), BF16)
        self.wgu = Ring(nc, "wgu", (128, 2, 8, 128), BF16, 3)
        self.wd = Ring(nc, "wd", (128, self.GS, 1024), BF16, 2)
        self.wF = Ring(nc, "wF", (128, 8, 128), BF16, 3)
        self.wTM = self._sb("wTM", (128, 8, NTM), BF16)
        self.wbig = Ring(nc, "wbig", (128, 8, 256), BF16, 2)
        self.wpp = self._sb("wpp", (128, 2, 1024), BF16)
        RXf = Ring(nc, "tmpx", (128, 512), F32, 9)
        RXb = Ring(nc, "tmpbx", (128, 512), BF16, 4)
        RYf = Ring(nc, "tmpy", (128, 512), F32, 7)
        RYb = Ring(nc, "tmpby", (128, 512), BF16, 12)
        padX = Ring(nc, "padx", (128, 3 + T), F32, 2)
        padY = Ring(nc, "pady", (128, 3 + T), F32, 2)
        self.cx_main = Ctx(RXf, RXb, padX, list(range(8)))
        self.cxX = Ctx(RXf, RXb, padX, [0, 1])
        self.cxY = Ctx(RYf, RYb, padY, [2, 3, 4])
        self.cx = self.cx_main
        self.xs = Ring(nc, "xs", (128, 1024), BF16, 2)
        zf_b, zf_k = [], []
        for j in range(7):
            zf_b.append(self.wd.bufs[j // 4][:, j % 4, :].bitcast(F32))
            zf_k.append(("wd", j // 4, j % 4))
        zb_b, zb_k = [], []
        for j in range(12):
            sl = j % 4
            zb_b.append(self.wgu.bufs[j // 4][:, sl // 2, (sl % 2) * 4:(sl % 2) * 4 + 4, :].rearrange("p a b -> p (a b)"))
            zb_k.append(("wgu", j // 4, sl))
        self.cxZ = Ctx(ViewRing("zf", zf_b, zf_k), ViewRing("zb", zb_b, zb_k), None, [5, 6, 7])
        self.small = self._sb("small", (128, 64), F32)
        self.junk = self._sb("junk", (128, 1024), BF16)
        self.q_sb = self._sb("q_sb", (128, 4, T), BF16)
        self.qT_dn = self._sb("qT_dn", (128, 2, T), BF16)
        self.kT_dn = self._sb("kT_dn", (128, 2, T), BF16)
        self.vT_dn = self._sb("vT_dn", (128, 2, T), BF16)
        self.kbg = self._sb("kbg", (128, NB, 256), BF16)
        self.kdec = self._sb("kdec", (128, NB, 256), BF16)
        self.vb = self._sb("vb", (128, NB, 256), BF16)
        self.gz = self._sb("gz", (128, NB, 256), F32)
        self.ba = self._sb("ba", (128, NB, 8), F32)
        self.sc = self._sb("sc", (128, 12, NB, 4), F32)
        self.EG = self._sb("EG", (128, 4, 128), F32)
        self.attnT = self._sb("attnT", (128, 4, 128), BF16)
        self.Ttb = self._sb("Ttb", (128, 4, 128), BF16)
        self.u_sb = self._sb("u_sb", (128, 256), F32)
        self.wT = self._sb("wT", (128, 2, 128), BF16)
        self.qgT = self._sb("qgT", (128, 2, 128), BF16)
        self.bsY = dict(EG=self.EG, EGk="EG", attnT=self.attnT, attnTk="attnT", Ttb=self.Ttb, Ttbk="Ttb",
                        u_sb=self.u_sb, u_sbk="u_sb", wT=self.wT, wTk=("wT",), qgT=self.qgT, qgTk=("qgT",))
        h4 = lambda ap: ap.rearrange("p (h i) -> p h i", h=4)
        self.bsZ = dict(EG=h4(self.wd.bufs[1][:, 3, :].bitcast(F32)), EGk=("wd", 1, 3),
                        attnT=h4(self.hT[:, 0, :]), attnTk=("hT", 0), Ttb=h4(self.hT[:, 1, :]), Ttbk=("hT", 1),
                        u_sb=self.hT[:, 2, :].bitcast(F32), u_sbk=("hT", 2),
                        wT=self.hT[:, 3, 0:256].rearrange("p (c i) -> p c i", c=2), wTk=("hT", 3, 0),
                        qgT=self.hT[:, 3, 256:512].rearrange("p (c i) -> p c i", c=2), qgTk=("hT", 3, 1))
        self.pbf = self._sb("pbf", (128, NB, PLE), BF16)
        self.pT = self._sb("pT", (128, 2, T), BF16)

    @property
    def tmp(self):
        return self.cx.tmp

    @property
    def tmpb(self):
        return self.cx.tmpb

    @property
    def pad(self):
        return self.cx.pad

    def psum(self):
        cx = self.cx
        i = cx.banks[cx.i]
        cx.i = (cx.i + 1) % len(cx.banks)
        return self.ps[i], ("ps", i)

    def interleave(self, streams, weights=None):
        fired = set()
        active = [[cx, g, (weights[i] if weights else 1), None] for i, (cx, g) in enumerate(streams)]
        while active:
            progressed = False
            for item in list(active):
                cx, g, wgt, _ = item
                self.cx = cx
                for _n in range(wgt):
                    if item[3] is not None:
                        if item[3] in fired:
                            item[3] = None
                        else:
                            break
                    try:
                        v = next(g)
                    except StopIteration:
                        active.remove(item)
                        progressed = True
                        break
                    progressed = True
                    if isinstance(v, tuple):
                        if v[0] == "set":
                            fired.add(v[1])
                        elif v[0] == "wait" and v[1] not in fired:
                            item[3] = v[1]
                            break
            assert progressed, "interleave deadlock"
        self.cx = self.cx_main

    def mm(self, out, lhsT, rhs, start, stop, r, w):
        return self.s.add("pe", lambda e: e.matmul(out, lhsT=lhsT, rhs=rhs, start=start, stop=stop,
                                                   skip_group_check=True), r, w)

    def tr(self, out, in_, ident, r, w):
        return self.s.add("pe", lambda e: e.transpose(out=out, in_=in_, identity=ident), r, w)

    def act(self, out, in_, func, r, w, bias=None, scale=None, accum_out=None):
        kw = {}
        if bias is not None:
            kw["bias"] = bias
        if scale is not None:
            kw["scale"] = scale
        if accum_out is not None:
            kw["accum_out"] = accum_out
        return self.s.add("act", lambda e: e.activation(out=out, in_=in_, func=func, **kw), r, w)

    def tt(self, eng, out, in0, in1, op, r, w):
        return self.s.add(eng, lambda e: e.tensor_tensor(out=out, in0=in0, in1=in1, op=op), r, w)

    def ts(self, eng, out, in0, s1, s2, op0, op1, r, w):
        if s2 is None:
            return self.s.add(eng, lambda e: e.tensor_scalar(out=out, in0=in0, scalar1=s1, scalar2=None, op0=op0), r, w)
        return self.s.add(eng, lambda e: e.tensor_scalar(out=out, in0=in0, scalar1=s1, scalar2=s2,
                                                         op0=op0, op1=op1), r, w)

    def stt(self, eng, out, in0, scalar, in1, op0, op1, r, w):
        return self.s.add(eng, lambda e: e.scalar_tensor_tensor(out=out, in0=in0, scalar=scalar, in1=in1,
                                                                op0=op0, op1=op1), r, w)

    def sig3(self, out, in_, r, w, nbias=None, nscale=-1.0):
        self.act(out, in_, AF.Exp, r, w, bias=nbias, scale=nscale)
        yield
        self.act(out, out, AF.Ln, w, w, bias=1.0)
        yield
        self.act(out, out, AF.Exp, w, w, scale=-1.0)

    def sig3_now(self, *a, **kw):
        for _ in self.sig3(*a, **kw):
            pass

    def cp(self, eng, out, in_, r, w):
        if eng == "act":
            return self.s.add("act", lambda e: e.copy(out=out, in_=in_), r, w)
        return self.s.add(eng, lambda e: e.tensor_copy(out=out, in_=in_), r, w)

    def dma(self, q, out, in_, r, w):
        return self.s.add(q, lambda e: e.dma_start(out=out, in_=in_), r, w, dma=True)

    def memset(self, eng, ap, val, w):
        return self.s.add(eng, lambda e: e.memset(ap, val), [], w)

    def dbg(self, name, ap, key, dtype=F32):
        if name not in self.dump:
            return
        shape = list(ap.shape)
        d = self.nc.dram_tensor("dbg_" + name, shape, dtype, kind="ExternalOutput").ap()
        op = self.dma("sp", d, ap, [key], [])
        self.dump_ops.append(op)

    def setup(self):
        L = DEPTH
        self.dma("sp", self.cst, self.cst_d, [], ["cst"])
        self.dma("sp", self.abias, self.abias_d.rearrange("p (a b n) -> p a b n", a=2, b=2), [], ["abias"])
        self.dma("sp", self.pp, self.pp_d.rearrange("l p n -> p l n"), [], ["pp"])
        self.dma("sp", self.rp, self.rp_d.rearrange("l p n -> p l n"), [], ["rp"])
        self.dma("sp", self.fing, self.fing_d, [], ["fing"])
        self.dma("pool", self.wab, self.wab_d.rearrange("l p (a c n) -> p l a c n", a=2, c=2), [], ["wab"])
        self.cp("dve", self.ident_b, self.ident_f, ["cst"], ["cstb"])
        self.cp("dve", self.ones_b, self.ones_f, ["cst"], ["cstb"])
        self.cp("dve", self.bones_b, self.cst[:, 512:640], ["cst"], ["cstb"])
        for l in range(L):
            der = self.der[:, l, :]
            lam = self.pp[:, l, 46:48]
            self.act(der[:, 16:18], lam, AF.Exp, ["pp"], [("der", l)], scale=-1.0)
            self.act(der[:, 16:18], der[:, 16:18], AF.Ln, [("der", l)], [("der", l)], bias=1.0)
            self.ts("dve", der[:, 0:2], der[:, 16:18], -8.0, None, ALU.mult, None, [("der", l)], [("der", l)])
            self.ts("dve", der[:, 2:4], der[:, 16:18], -16.0, None, ALU.mult, None, [("der", l)], [("der", l)])
            self.act(der[:, 4:8], self.rp[:, l, 8:12], AF.Exp, ["rp"], [("der", l)])
            self.ts("dve", der[:, 4:8], der[:, 4:8], -1.0, None, ALU.mult, None, [("der", l)], [("der", l)])
            self.ts("dve", der[:, 24:26], self.pp[:, l, 42:44], -1.0, None, ALU.mult, None, ["pp"], [("der", l)])
            self.ts("dve", der[:, 26:28], self.pp[:, l, 44:46], -1.0, None, ALU.mult, None, ["pp"], [("der", l)])
            self.act(der[:, 8:16], self.rp[:, l, 0:8], AF.Exp, ["rp"], [("der", l)])

    def zero_state(self):
        for l in range(self.L):
            self.memset("dve", self.halo_lru[:, l], 0.0, [("halo_lru", l)])
            self.memset("dve", self.halo_dn[:, l], 0.0, [("halo_dn", l)])
            self.memset("dve", self.kTpad[:, l, 0:128], 0.0, [("kTpad", l, "h")])
            self.memset("dve", self.vpad[:, l, 0, :], 0.0, [("vpad", l, "h")])
            self.memset("dve", self.S[:, l], 0.0, [("S", l)])
            self.memset("dve", self.Sb[:, l], 0.0, [("Sb", l)])
            self.memset("dve", self.lru_h[:, l], 0.0, [("lru_h", l)])

    def norm_T(self, l, j):
        NB = self.NB
        ss = self.small[:, 0:NB]
        for m in range(NB):
            self.act(self.junk, self.h[:, m, :], AF.Square, [("h", m)], [("small", "ss", m), "junk"],
                     accum_out=self.small[:, m:m + 1])
        self.act(ss, ss, AF.Ln, [("small", "ss")], [("small", "ss")], bias=EPS, scale=1.0 / D)
        self.act(ss, ss, AF.Exp, [("small", "ss")], [("small", "ss")], scale=-0.5)
        gam = self.pp[:, l, j * 8:(j + 1) * 8].unsqueeze(2).to_broadcast([128, 8, 128])
        for m in range(NB):
            xs = self.xs.get()
            self.act(xs.ap, self.h[:, m, :], AF.Copy, [("h", m), ("small", "ss")], [xs.key],
                     scale=self.small[:, m:m + 1])
            pt, pk = self.psum()
            pv = pt.bitcast(BF16).rearrange("p (c t) -> p c t", c=8)
            for c in range(8):
                self.tr(pv[:, c, :], xs.ap[:, c * 128:(c + 1) * 128], self.ident_b, [xs.key, "cstb"], [pk])
            self.tt("dve", self.xnT[:, :, m * 128:(m + 1) * 128], pv, gam, ALU.mult, [pk, "pp"], [("xnT", m)])

    def ffn(self, l, fi):
        NB, T = self.NB, self.T
        base = (fi * DEPTH + l) * NFC
        groups = []
        c0 = 0
        while c0 < NFC:
            n = min(self.GS, NFC - c0)
            groups.append(list(range(c0, c0 + n)))
            c0 += n
        for grp in groups:
            wd = self.wd.get()
            n = len(grp)
            self.dma("pool", wd.ap[:, 0:n, :],
                     self.wd_d[base + grp[0]:base + grp[0] + n].rearrange("c p n -> p c n"), [], [wd.key])
            for jj, c in enumerate(grp):
                wg = self.wgu.get()
                self.dma("pool", wg.ap, self.wgu_d[base + c].rearrange("p (a k n) -> p a k n", a=2, k=8),
                         [], [wg.key])
                pg, kg = self.psum()
                pu, ku = self.psum()
                for k in range(8):
                    self.mm(pg, wg.ap[:, 0, k, :], self.xnT[:, k, :], k == 0, k == 7, [wg.key, ("xnT",)], [kg])
                for k in range(8):
                    self.mm(pu, wg.ap[:, 1, k, :], self.xnT[:, k, :], k == 0, k == 7, [wg.key, ("xnT",)], [ku])
                sg = self.tmp.get()
                self.act(sg.ap, pg, AF.Silu, [kg], [sg.key])
                self.tt("dve", self.hT[:, jj, :], sg.ap, pu, ALU.mult, [sg.key, ku], [("hT", jj)])
            for m in range(NB):
                for dh in range(2):
                    po, ko = self.psum()
                    for jj in range(n):
                        self.mm(po, self.hT[:, jj, m * 128:(m + 1) * 128], wd.ap[:, jj, dh * 512:(dh + 1) * 512],
                                jj == 0, jj == n - 1, [("hT", jj), wd.key], [ko])
                    hs = self.h[:, m, dh * 512:(dh + 1) * 512]
                    self.stt("dve", hs, po, 0.5, hs, ALU.mult, ALU.add, [ko, ("h", m, dh)], [("h", m, dh)])

    def inproj_F(self, l, i):
        wf = self.wF.get()
        self.dma("pool", wf.ap, self.winF_d[l * NF + i].rearrange("p (k n) -> p k n", k=8), [], [wf.key])
        pf, kf = self.psum()
        for k in range(8):
            self.mm(pf, wf.ap[:, k, :], self.xnT[:, k, :], k == 0, k == 7, [wf.key, ("xnT",)], [kf])
        return pf, kf

    def conv(self, eng, pad, wcols, bias, l):
        T = self.T
        o = self.tmp.get()
        w = lambda k: wcols[:, k:k + 1]
        self.ts(eng, o.ap, pad.ap[:, 0:T], w(0), 0.0 if bias is None else bias, ALU.mult, ALU.add,
                [pad.key, "pp"], [o.key])
        for k in range(1, 4):
            self.stt(eng, o.ap, pad.ap[:, k:k + T], w(k), o.ap, ALU.mult, ALU.add, [pad.key, "pp", o.key], [o.key])
        return o

    def lru(self, l, c):
        T = self.T
        pp = self.pp[:, l, :]
        der = self.der[:, l, :]
        pad = self.pad.get()
        yield
        self.cp("act", pad.ap[:, 0:3], self.halo_lru[:, l, c, :], [("halo_lru", l, c)], [pad.key])
        px, kx = self.inproj_F(l, c)
        yield
        self.cp("act", pad.ap[:, 3:3 + T], px, [kx], [pad.key])
        pgate, kgate = self.inproj_F(l, 2 + c)
        gsb = self.tmp.get()
        yield
        self.cp("act", gsb.ap, pgate, [kgate], [gsb.key])
        xr = self.conv("dve", pad, pp[:, 32 + c * 4:36 + c * 4], pp[:, 40 + c:41 + c], l)
        yield
        self.cp("act", self.halo_lru[:, l, c, :], pad.ap[:, T:T + 3], [pad.key], [("halo_lru", l, c)])
        xrb = self.tmpb.get()
        yield
        self.cp("act", xrb.ap, xr.ap, [xr.key], [xrb.key])
        pr, kr = self.psum()
        self.mm(pr, self.wab[:, l, 0, c, :], xrb.ap, True, True, ["wab", xrb.key], [kr])
        pi, ki = self.psum()
        self.mm(pi, self.wab[:, l, 1, c, :], xrb.ap, True, True, ["wab", xrb.key], [ki])
        r = self.tmp.get()
        yield
        yield from self.sig3(r.ap, pr, [kr, ("der", l)], [r.key], nbias=der[:, 24 + c:25 + c])
        ig = self.tmp.get()
        yield
        yield from self.sig3(ig.ap, pi, [ki, ("der", l)], [ig.key], nbias=der[:, 26 + c:27 + c])
        a = self.tmp.get()
        yield
        self.act(a.ap, r.ap, AF.Exp, [r.key, ("der", l)], [a.key], scale=der[:, c:c + 1])
        a2 = self.tmp.get()
        yield
        self.act(a2.ap, r.ap, AF.Exp, [r.key, ("der", l)], [a2.key], scale=der[:, 2 + c:3 + c])
        yield
        self.act(a2.ap, a2.ap, AF.Ln, [a2.key], [a2.key], bias=1.0, scale=-1.0)
        yield
        self.act(a2.ap, a2.ap, AF.Exp, [a2.key], [a2.key], scale=0.5)
        yield
        self.tt("dve", ig.ap, ig.ap, xr.ap, ALU.mult, [ig.key, xr.key], [ig.key])
        yield
        self.tt("dve", ig.ap, ig.ap, a2.ap, ALU.mult, [ig.key, a2.key], [ig.key])
        hs = self.tmp.get()
        st = self.lru_h[:, l, c:c + 1]
        hs_ap, a_ap, ig_ap = hs.ap, a.ap, ig.ap
        yield
        self.s.add("dve", lambda e: e.tensor_tensor_scan(out=hs_ap, data0=a_ap, data1=ig_ap, initial=st,
                                                         op0=ALU.mult, op1=ALU.add),
                   [a.key, ig.key, ("lru_h", l, c)], [hs.key])
        yield
        self.cp("dve", st, hs.ap[:, T - 1:T], [hs.key], [("lru_h", l, c)])
        sq = self.tmp.get()
        yield
        self.act(sq.ap, gsb.ap, AF.Square, [gsb.key], [sq.key])
        yield
        self.ts("dve", sq.ap, sq.ap, 0.044715, 1.0, ALU.mult, ALU.add, [sq.key], [sq.key])
        yield
        self.tt("dve", sq.ap, sq.ap, gsb.ap, ALU.mult, [sq.key, gsb.key], [sq.key])
        yield
        yield from self.sig3(sq.ap, sq.ap, [sq.key], [sq.key], nscale=-2.0 * math.sqrt(2.0 / math.pi))
        yield
        self.tt("dve", sq.ap, sq.ap, gsb.ap, ALU.mult, [sq.key, gsb.key], [sq.key])
        yield
        self.tt("dve", self.yT[:, c, :], sq.ap, hs.ap, ALU.mult, [sq.key, hs.key], [("yT", "l", c)])

    def inproj_TM(self, l):
        NB = self.NB
        self.dma("pool", self.wTM, self.winT_d[l].rearrange("p (k n) -> p k n", k=8), [], ["wTM"])
        for m in range(NB):
            pt, kt = self.psum()
            for k in range(8):
                self.mm(pt[:, 0:NTM], self.xnT[:, k, m * 128:(m + 1) * 128], self.wTM[:, k, :], k == 0, k == 7,
                        [("xnT", m), "wTM"], [kt])
            self.cp("act", self.vpad[:, l, 1 + m, :], pt[:, 0:128], [kt], [("vpad", l, "b", m)])
            self.sig3_now(self.gz[:, m, :], pt[:, 128:384], [kt], [("gz", m)])
            self.tt("dve", self.gz[:, m, :], self.gz[:, m, :], pt[:, 128:384], ALU.mult, [kt, ("gz", m)], [("gz", m)])
            self.cp("act", self.ba[:, m, :], pt[:, 384:392], [kt], [("ba", m)])
        dnn = self.rp[:, l, 16:80].unsqueeze(1).unsqueeze(1).to_broadcast([128, NB, 4, 64])
        gzv = self.gz.rearrange("p m (h d) -> p m h d", h=4)
        self.tt("dve", gzv, gzv, dnn, ALU.mult, [("gz",), "rp"], [("gz",)])

    def attention(self, l, gtile):
        NB, T = self.NB, self.T
        for i in range(4):
            pq, kq = self.inproj_F(l, 4 + i)
            yield
            self.s.add("act", (lambda o, p: lambda e: e.mul(out=o, in_=p, mul=0.125))(self.q_sb[:, i, :], pq),
                       [kq], [("q_sb", i)])
        pk_, kk = self.inproj_F(l, 8)
        yield
        self.cp("act", self.kTpad[:, l, 128:128 + T], pk_, [kk], [("kTpad", l, "b")])
        esink = self.der[:, l, 8:16]
        for b in range(NB):
            first = (gtile * NB + b == 0)
            for kh in range(2):
                lo, hi = kh * 64, kh * 64 + 64
                es = []
                for kb in ((1,) if first else (0, 1)):
                    col0 = (b + kb) * 128
                    pS, kS = self.psum()
                    self.mm(pS.rearrange("p (g i) -> p g i", g=4), self.kTpad[lo:hi, l, col0:col0 + 128],
                            self.q_sb[lo:hi, :, b * 128:(b + 1) * 128], True, True,
                            [("kTpad", l), ("q_sb",)], [kS])
                    ein = self.tmp.get()
                    yield
                    self.tt("dve", ein.ap, pS, self.abias[:, kh, kb, :], ALU.add, [kS, "abias"], [ein.key])
                    e = self.tmpb.get()
                    yield
                    self.act(e.ap, ein.ap, AF.Exp, [ein.key], [e.key])
                    es.append((kb, e))
                pden, kden = self.psum()
                for n_, (kb, e) in enumerate(es):
                    self.mm(pden, self.ones_b, e.ap, n_ == 0, n_ == len(es) - 1, ["cstb", e.key], [kden])
                po, ko = self.psum()
                for n_, (kb, e) in enumerate(es):
                    self.mm(po, self.vpad[:, l, b + kb, :], e.ap, n_ == 0, n_ == len(es) - 1,
                            [("vpad", l), e.key], [ko])
                rd = self.tmp.get()
                rdv = rd.ap.rearrange("p (g i) -> p g i", g=4)[lo:hi]
                pdv = pden.rearrange("p (g i) -> p g i", g=4)[lo:hi]
                pov = po.rearrange("p (g i) -> p g i", g=4)[lo:hi]
                esb = esink[lo:hi, kh * 4:(kh + 1) * 4].unsqueeze(2).to_broadcast([64, 4, 128])
                yield
                self.tt("dve", rdv, pdv, esb, ALU.add, [kden, ("der", l)], [rd.key])
                yield
                self.act(rdv, rdv, AF.Ln, [rd.key], [rd.key])
                yield
                self.act(rdv, rdv, AF.Exp, [rd.key], [rd.key], scale=-1.0)
                yield
                self.tt("dve", self.yT[lo:hi, 2:6, b * 128:(b + 1) * 128], pov, rdv, ALU.mult,
                        [ko, rd.key], [("yT", "a", b, kh)])
        yield
        self.cp("act", self.kTpad[:, l, 0:128], self.kTpad[:, l, T:T + 128], [("kTpad", l, "b")], [("kTpad", l, "h")])
        yield
        self.cp("act", self.vpad[:, l, 0, :], self.vpad[:, l, NB, :], [("vpad", l, "b", NB - 1)], [("vpad", l, "h")])

    def dn_pre(self, l):
        T = self.T
        pp = self.pp[:, l, :]
        for i in range(6):
            pad = self.pad.get()
            yield
            self.cp("act", pad.ap[:, 0:3], self.halo_dn[:, l, i, :], [("halo_dn", l, i)], [pad.key])
            pf, kf = self.inproj_F(l, 9 + i)
            yield
            self.cp("act", pad.ap[:, 3:3 + T], pf, [kf], [pad.key])
            cv = self.conv("dve", pad, pp[:, 48 + i * 4:52 + i * 4], None, l)
            yield
            self.cp("act", self.halo_dn[:, l, i, :], pad.ap[:, T:T + 3], [pad.key], [("halo_dn", l, i)])
            kind, c = divmod(i, 2)
            if kind == 2:
                yield
                sv = self.tmp.get()
                yield from self.sig3(sv.ap, cv.ap, [cv.key], [sv.key])
                yield
                self.tt("dve", self.vT_dn[:, c, :], cv.ap, sv.ap, ALU.mult, [cv.key, sv.key], [("vT_dn", c)])
                continue
            sv = self.tmp.get()
            yield from self.sig3(sv.ap, cv.ap, [cv.key], [sv.key])
            yield
            self.tt("dve", cv.ap, cv.ap, sv.ap, ALU.mult, [cv.key, sv.key], [cv.key])
            sqb = self.tmpb.get()
            yield
            self.act(sqb.ap, cv.ap, AF.Square, [cv.key], [sqb.key])
            pn, kn = self.psum()
            self.mm(pn, self.bones_b, sqb.ap, True, True, ["cstb", sqb.key], [kn])
            rn = self.tmp.get()
            yield
            self.act(rn.ap, pn, AF.Ln, [kn], [rn.key], bias=EPS)
            yield
            self.act(rn.ap, rn.ap, AF.Exp, [rn.key], [rn.key], scale=-0.5)
            if kind == 0:
                yield
                self.stt("dve", self.qT_dn[:, c, :], cv.ap, 0.125, rn.ap, ALU.mult, ALU.mult,
                         [cv.key, rn.key], [("qT_dn", c)])
            else:
                yield
                self.tt("dve", self.kT_dn[:, c, :], cv.ap, rn.ap, ALU.mult, [cv.key, rn.key], [("kT_dn", c)])

    def dn_scalars(self, l):
        NB = self.NB
        sc = self.sc
        BETA, X, AX_, EX, G, GC, NGC, EGC, CKBG, CDEC, SDEC, RX = range(12)
        self.BETA, self.G, self.GC, self.NGC, self.CKBG, self.CDEC, self.SDEC = BETA, G, GC, NGC, CKBG, CDEC, SDEC
        k = lambda i: ("sc", i)
        b_ = self.ba[:, :, 0:4]
        a_ = self.ba[:, :, 4:8]
        self.sig3_now(sc[:, BETA], b_, [("ba",)], [k(BETA)])
        dtb = self.rp[:, l, 12:16].unsqueeze(1).to_broadcast([128, NB, 4])
        self.tt("dve", sc[:, X], a_, dtb, ALU.add, [("ba",), "rp"], [k(X)])
        self.act(sc[:, AX_], sc[:, X], AF.Abs, [k(X)], [k(AX_)])
        self.act(sc[:, EX], sc[:, AX_], AF.Exp, [k(AX_)], [k(EX)], scale=-1.0)
        self.act(sc[:, EX], sc[:, EX], AF.Ln, [k(EX)], [k(EX)], bias=1.0)
        self.ts("dve", sc[:, RX], sc[:, X], 0.0, None, ALU.max, None, [k(X)], [k(RX)])
        self.tt("dve", sc[:, RX], sc[:, RX], sc[:, EX], ALU.add, [k(RX), k(EX)], [k(RX)])
        negA = self.der[:, l, 4:8].unsqueeze(1).to_broadcast([128, NB, 4])
        self.tt("dve", sc[:, G], sc[:, RX], negA, ALU.mult, [k(RX), ("der", l)], [k(G)])
        pc, kc = self.psum()
        for b in range(NB):
            self.mm(pc[:, b * 4:(b + 1) * 4], self.U_f, sc[:, G, b, :], True, True, ["cst", k(G)], [kc])
        self.cp("act", sc[:, GC].rearrange("p b h -> p (b h)"), pc[:, 0:4 * NB], [kc], [k(GC)])
        self.ts("dve", sc[:, NGC], sc[:, GC], -1.0, None, ALU.mult, None, [k(GC)], [k(NGC)])
        self.act(sc[:, EGC], sc[:, GC], AF.Exp, [k(GC)], [k(EGC)])
        self.tt("dve", sc[:, CKBG], sc[:, EGC], sc[:, BETA], ALU.mult, [k(EGC), k(BETA)], [k(CKBG)])

    def dn_block(self, l, b, bs, mid):
        tdt = self.tdt
        sc = self.sc
        k = lambda i: ("sc", i)
        blk = slice(b * 128, (b + 1) * 128)
        v4 = lambda ap: ap.rearrange("p (h j) -> p h j", h=4)
        ug = self.tmp.get()
        yield
        self.tt("dve", v4(ug.ap), self.U_f.unsqueeze(1).to_broadcast([128, 4, 128]),
                sc[:, self.G, b, :].unsqueeze(2).to_broadcast([128, 4, 128]), ALU.mult,
                ["cst", k(self.G)], [ug.key])
        pG, kG = self.psum()
        self.mm(pG, self.ones_f, ug.ap, True, True, ["cst", ug.key], [kG])
        pG4 = v4(pG)
        ngb = self.tmp.get()
        yield
        self.tt("dve", v4(ngb.ap), self.ones_f.unsqueeze(1).to_broadcast([128, 4, 128]),
                sc[:, self.NGC, b, :].unsqueeze(2).to_broadcast([128, 4, 128]), ALU.mult,
                ["cst", k(self.NGC)], [ngb.key])
        pD, kD = self.psum()
        self.mm(pD, self.ones_f, ug.ap, True, False, ["cst", ug.key], [kD])
        self.mm(pD, self.ident_f, ngb.ap, False, True, ["cst", ngb.key], [kD])
        yield
        self.act(bs["EG"], pG4, AF.Exp, [kG], [bs["EGk"]])
        r1 = self.tmp.get()
        r2 = self.tmp.get()
        yield
        self.act(r1.ap, pD, AF.Relu, [kD], [r1.key], scale=-1.0)
        yield
        self.act(r2.ap, pD, AF.Relu, [kD], [r2.key])
        yield
        self.act(r1.ap, r1.ap, AF.Exp, [r1.key], [r1.key], scale=-1.0)
        yield
        self.act(r2.ap, r2.ap, AF.Exp, [r2.key], [r2.key], scale=-1.0)
        yield
        self.tt("dve", sc[:, self.CDEC, b, :], pG4[:, :, 127], sc[:, self.GC, b, :], ALU.subtract,
                [kG, k(self.GC)], [("sc", self.CDEC, b)])
        yield
        self.act(sc[:, self.CDEC, b, :], sc[:, self.CDEC, b, :], AF.Exp, [("sc", self.CDEC, b)], [("sc", self.CDEC, b)])
        yield
        self.cp("dve", sc[:, self.SDEC, b, :], bs["EG"][:, :, 127], [bs["EGk"]], [("sc", self.SDEC, b)])
        if self.blk_stop <= 1:
            return
        pkt, kkt = self.psum()
        pkb = pkt.bitcast(BF16)
        for c in range(2):
            self.tr(pkb[:, c * 128:(c + 1) * 128], self.kT_dn[:, c, blk], self.ident_b, [("kT_dn", c), "cstb"], [kkt])
        for c in range(2):
            self.tr(pkb[:, 256 + c * 128:256 + (c + 1) * 128], self.vT_dn[:, c, blk], self.ident_b,
                    [("vT_dn", c), "cstb"], [kkt])
        h64 = lambda ap: ap.rearrange("p (h d) -> p h d", h=4)
        bc = lambda i: sc[:, i, b, :].unsqueeze(2).to_broadcast([128, 4, 64])
        import os
        var = os.environ.get("BLK_VAR", "abc")
        if "a" in var:
            yield
            self.tt("dve", h64(self.kbg[:, b, :]), h64(pkb[:, 0:256]), bc(self.CKBG), ALU.mult,
                    [kkt, k(self.CKBG)], [("kbg", b)])
        if "b" in var:
            yield
            self.tt("dve", h64(self.kdec[:, b, :]), h64(pkb[:, 0:256]), bc(self.CDEC), ALU.mult,
                    [kkt, ("sc", self.CDEC, b)], [("kdec", b)])
        if "c" in var:
            yield
            self.tt("dve", h64(self.vb[:, b, :]), h64(pkb[:, 256:512]), bc(self.BETA), ALU.mult,
                    [kkt, k(self.BETA)], [("vb", b)])
        if self.blk_stop <= 2:
            return
        pKK = [self.psum(), self.psum()]
        v22 = lambda ap: ap.rearrange("p (c hh j) -> p c hh j", c=2, hh=2)
        c2 = lambda ap, n: ap[:, 0:2 * n].rearrange("p (c j) -> p c j", c=2)
        for h in range(4):
            lo, c = (h % 2) * 64, h // 2
            self.mm(pKK[h % 2][0][:, c * 128:(c + 1) * 128], self.kT_dn[lo:lo + 64, c, blk],
                    self.kT_dn[lo:lo + 64, c, blk], True, True, [("kT_dn", c)], [pKK[h % 2][1]])
        yield
        self.tt("dve", v4(r2.ap), v4(r2.ap), self.SL_f.unsqueeze(1).to_broadcast([128, 4, 128]), ALU.mult,
                [r2.key, "cst"], [r2.key])
        for hh in range(2):
            yield
            self.tt("dve", v22(r2.ap)[:, :, hh, :], v22(r2.ap)[:, :, hh, :], c2(pKK[hh][0], 128), ALU.mult,
                    [r2.key, pKK[hh][1]], [r2.key])
        P = self.tmp.get()
        yield
        self.tt("dve", v4(P.ap), v4(r2.ap), sc[:, self.BETA, b, :].unsqueeze(2).to_broadcast([128, 4, 128]),
                ALU.mult, [r2.key, k(self.BETA)], [P.key])
        yield
        self.tt("dve", v4(r1.ap), v4(r1.ap), self.U_f.unsqueeze(1).to_broadcast([128, 4, 128]), ALU.mult,
                [r1.key, "cst"], [r1.key])
        pKQ = [self.psum(), self.psum()]
        for h in range(4):
            lo, c = (h % 2) * 64, h // 2
            self.mm(pKQ[h % 2][0][:, c * 128:(c + 1) * 128], self.kT_dn[lo:lo + 64, c, blk],
                    self.qT_dn[lo:lo + 64, c, blk], True, True, [("kT_dn", c), ("qT_dn", c)], [pKQ[h % 2][1]])
        at22 = bs["attnT"].rearrange("p (c hh) i -> p c hh i", c=2)
        for hh in range(2):
            yield
            self.tt("dve", at22[:, :, hh, :], v22(r1.ap)[:, :, hh, :], c2(pKQ[hh][0], 128), ALU.mult,
                    [r1.key, pKQ[hh][1]], [bs["attnTk"]])
        if self.blk_stop <= 3:
            return
        Pb = self.tmpb.get()
        yield
        self.cp("act", Pb.ap, P.ap, [P.key], [Pb.key])
        pPt, kPt = self.psum()
        for h in range(4):
            self.tr(v4(pPt)[:, h, :], v4(P.ap)[:, h, :], self.ident_f, [P.key, "cst"], [kPt])
        Ptb = self.tmpb.get()
        yield
        self.cp("act", Ptb.ap, pPt, [kPt], [Ptb.key])
        Tt = self.tmpb.get()
        yield
        self.tt("dve", v4(Tt.ap), self.ident_f.unsqueeze(1).to_broadcast([128, 4, 128]), v4(pPt), ALU.subtract,
                ["cst", kPt], [Tt.key])
        if self.blk_stop <= 4:
            return
        pM, kM = self.psum()
        pN, kN = self.psum()
        for h in range(4):
            self.mm(v4(pM)[:, h, :], v4(Ptb.ap)[:, h, :], v4(Pb.ap)[:, h, :], True, True, [Ptb.key, Pb.key], [kM])
        for h in range(4):
            self.mm(v4(pN)[:, h, :], v4(Pb.ap)[:, h, :], v4(Ptb.ap)[:, h, :], True, True, [Ptb.key, Pb.key], [kN])
        M = self.tmpb.get()
        N = self.tmpb.get()
        yield
        self.cp("act", M.ap, pM, [kM], [M.key])
        yield
        self.cp("act", N.ap, pN, [kN], [N.key])
        for r in range(1, 7):
            pX, kX = self.psum()
            for h in range(4):
                self.mm(v4(pX)[:, h, :], v4(M.ap)[:, h, :], v4(Tt.ap)[:, h, :], True, True, [M.key, Tt.key], [kX])
            M2 = N2 = None
            if r + 1 <= 6:
                pM, kM = self.psum()
                for h in range(4):
                    self.mm(v4(pM)[:, h, :], v4(N.ap)[:, h, :], v4(M.ap)[:, h, :], True, True, [M.key, N.key], [kM])
                M2 = self.tmpb.get()
                yield
                self.cp("act", M2.ap, pM, [kM], [M2.key])
            if r + 1 <= 5:
                pN, kN = self.psum()
                for h in range(4):
                    self.mm(v4(pN)[:, h, :], v4(M.ap)[:, h, :], v4(N.ap)[:, h, :], True, True, [M.key, N.key], [kN])
                N2 = self.tmpb.get()
                yield
                self.cp("act", N2.ap, pN, [kN], [N2.key])
            if r < 6:
                Tt2 = self.tmpb.get()
                yield
                self.tt("dve", Tt2.ap, Tt.ap, pX, ALU.add, [Tt.key, kX], [Tt2.key])
                Tt = Tt2
            else:
                yield
                self.tt("dve", bs["Ttb"].rearrange("p h i -> p (h i)"), Tt.ap, pX, ALU.add, [Tt.key, kX], [bs["Ttbk"]])
            M, N = M2, N2
        if self.blk_stop <= 5:
            return
        pu, ku = self.psum()
        for h in range(4):
            self.mm(pu[:, h * 64:(h + 1) * 64], bs["Ttb"][:, h, :], self.vb[:, b, h * 64:(h + 1) * 64], True, True,
                    [bs["Ttbk"], ("vb", b)], [ku])
        yield
        self.cp("act", bs["u_sb"], pu[:, 0:256], [ku], [bs["u_sbk"]])
        pw, kw = self.psum()
        for h in range(4):
            c = h // 2
            self.mm(v4(pw)[:, h, :], self.kbg[:, b, c * 128:(c + 1) * 128], bs["Ttb"][:, h, :], True, True,
                    [bs["Ttbk"], ("kbg", b)], [kw])
        pw22 = pw.rearrange("p (c hh i) -> p c hh i", c=2, hh=2)
        yield
        self.cp("act", bs["wT"][0:64], pw22[0:64, :, 0, :], [kw], [bs["wTk"] + (0,)])
        yield
        self.cp("act", bs["wT"][64:128], pw22[64:128, :, 1, :], [kw], [bs["wTk"] + (1,)])
        eg22 = bs["EG"].rearrange("p (c hh) i -> p c hh i", c=2)
        yield
        self.tt("dve", bs["qgT"][0:64], self.qT_dn[0:64, :, blk], eg22[0:64, :, 0, :], ALU.mult,
                [("qT_dn",), bs["EGk"]], [bs["qgTk"] + (0,)])
        yield
        self.tt("dve", bs["qgT"][64:128], self.qT_dn[64:128, :, blk], eg22[64:128, :, 1, :], ALU.mult,
                [("qT_dn",), bs["EGk"]], [bs["qgTk"] + (1,)])
        if self.blk_stop <= 6:
            return
        if b > 0:
            yield ("wait", ("seq", mid, b - 1))
        S = self.S[:, l]
        Sb = self.Sb[:, l]
        pws = [self.psum(), self.psum()]
        for h in range(4):
            lo, c = (h % 2) * 64, h // 2
            self.mm(pws[h % 2][0][:, c * 64:(c + 1) * 64], bs["wT"][lo:lo + 64, c, :], Sb[lo:lo + 64, c, :], True, True,
                    [bs["wTk"], ("Sb", l)], [pws[h % 2][1]])
        vn = self.tmpb.get()
        vnew = vn.ap[:, 0:256]
        q64 = lambda ap: ap.rearrange("p (c hh d) -> p c hh d", c=2, hh=2)
        for hh in range(2):
            yield
            self.tt("dve", q64(vnew)[:, :, hh, :], q64(bs["u_sb"])[:, :, hh, :], c2(pws[hh][0], 64), ALU.subtract,
                    [bs["u_sbk"], pws[hh][1]], [vn.key])
        po = [self.psum(), self.psum()]
        for h in range(4):
            lo, c = (h % 2) * 64, h // 2
            self.mm(po[h % 2][0][:, c * 64:(c + 1) * 64], bs["qgT"][lo:lo + 64, c, :], Sb[lo:lo + 64, c, :], True, False,
                    [bs["qgTk"], ("Sb", l)], [po[h % 2][1]])
            self.mm(po[h % 2][0][:, c * 64:(c + 1) * 64], bs["attnT"][:, h, :], vnew[:, h * 64:(h + 1) * 64], False, True,
                    [bs["attnTk"], vn.key], [po[h % 2][1]])
        pst, kst = self.psum()
        for h in range(4):
            c = h // 2
            self.mm(pst[:, h * 64:(h + 1) * 64], self.kdec[:, b, c * 128:(c + 1) * 128], vnew[:, h * 64:(h + 1) * 64],
                    True, True, [("kdec", b), vn.key], [kst])
        for h in range(4):
            lo, c = (h % 2) * 64, h // 2
            yield
            self.stt("dve", S[lo:lo + 64, c, :], S[lo:lo + 64, c, :], sc[lo:lo + 64, self.SDEC, b, h:h + 1],
                     pst[lo:lo + 64, h * 64:(h + 1) * 64], ALU.mult, ALU.add,
                     [("S", l), ("sc", self.SDEC, b), kst], [("S", l)])
        yield
        self.cp("act", Sb, S, [("S", l)], [("Sb", l)])
        yield ("set", ("seq", mid, b))
        if self.blk_stop <= 7:
            return
        o = self.tmp.get()
        ov = o.ap[:, 0:256]
        for hh in range(2):
            yield
            self.cp("act", q64(ov)[:, :, hh, :], c2(po[hh][0], 64), [po[hh][1]], [o.key])
        sq = self.tmp.get()
        sqv = sq.ap[:, 0:256]
        yield
        self.tt("dve", sqv, ov, ov, ALU.mult, [o.key], [sq.key])
        ss4 = sq.ap[:, 256:260]
        yield
        self.s.add("dve", lambda e: e.tensor_reduce(out=ss4, in_=sqv.rearrange("p (h d) -> p h d", h=4),
                                                    axis=AX.X, op=ALU.add), [sq.key], [sq.key])
        yield
        self.act(ss4, ss4, AF.Ln, [sq.key], [sq.key], bias=EPS, scale=1.0 / 64)
        yield
        self.act(ss4, ss4, AF.Exp, [sq.key], [sq.key], scale=-0.5)
        yield
        self.tt("dve", h64(ov), h64(ov), ss4.unsqueeze(2).to_broadcast([128, 4, 64]), ALU.mult,
                [o.key, sq.key], [o.key])
        yd = self.tmpb.get()
        yield
        self.tt("dve", yd.ap[:, 0:256], ov, self.gz[:, b, :], ALU.mult, [o.key, ("gz",)], [yd.key])
        pT_, kT_ = self.psum()
        pTb = pT_.bitcast(BF16)
        for c in range(2):
            self.tr(pTb[:, c * 128:(c + 1) * 128], yd.ap[:, c * 128:(c + 1) * 128], self.ident_b, [yd.key, "cstb"], [kT_])
        yield
        self.cp("act", self.yT[:, 6:8, blk], pTb[:, 0:256].rearrange("p (c i) -> p c i", c=2), [kT_], [("yT", "d", b)])

    def outproj(self, l):
        NB = self.NB
        for dq in range(4):
            w = self.wbig.get()
            self.dma("pool", w.ap, self.wout_d[l].rearrange("p (k n) -> p k n", k=8)[:, :, dq * 256:(dq + 1) * 256],
                     [], [w.key])
            for m in range(NB):
                po, ko = self.psum()
                for c in range(8):
                    self.mm(po[:, 0:256], self.yT[:, c, m * 128:(m + 1) * 128], w.ap[:, c, :], c == 0, c == 7,
                            [("yT",), w.key], [ko])
                hs = self.h[:, m, dq * 256:(dq + 1) * 256]
                self.tt("dve", hs, hs, po[:, 0:256], ALU.add, [ko, ("h", m, dq // 2, dq % 2)], [("h", m, dq // 2, dq % 2)])

    def mixer(self, l, gtile):
        self.norm_T(l, 1)
        mx = self.mix

        self.mix_id += 1
        mid = self.mix_id

        def sx():
            if 'lru' in mx:
                for c in range(2):
                    yield from self.lru(l, c)
            if 'att' in mx:
                yield ("wait", ("tm", mid))
                yield from self.attention(l, gtile)

        def sy():
            if 'tm' in mx:
                self.inproj_TM(l)
            yield ("set", ("tm", mid))
            if 'sc' in mx:
                self.dn_scalars(l)
            yield
            if 'pre' in mx:
                yield from self.dn_pre(l)
            yield ("set", ("pre", mid))
            if 'blk' in mx:
                for b in range(0, self.NB, 2):
                    yield from self.dn_block(l, b, self.bsY, mid)

        def sz():
            yield ("wait", ("pre", mid))
            if 'blk' in mx:
                for b in range(1, self.NB, 2):
                    yield from self.dn_block(l, b, self.bsZ, mid)

        if self.ilv:
            self.interleave([(self.cxX, sx()), (self.cxY, sy()), (self.cxZ, sz())], weights=self.ilv_w)
        else:
            self.interleave([(self.cxX, sx())])
            self.interleave([(self.cxY, sy()), (self.cxZ, sz())])
        if 'out' in mx:
            self.outproj(l)

    def ple(self, l, seq, t0):
        NB, T = self.NB, self.T
        self.norm_T(l, 3)
        self.dma("pool", self.pbf, self.p_d[l, seq, t0:t0 + T, :].rearrange("(m p) n -> p m n", p=128), [], ["pbf"])
        self.dma("pool", self.wpp, self.wpp_d[l].rearrange("p (k n) -> p k n", k=2), [], ["wpp"])
        for m in range(NB):
            pt, kt = self.psum()
            pb = pt.bitcast(BF16)
            for c in range(2):
                self.tr(pb[:, c * 128:(c + 1) * 128], self.pbf[:, m, c * 128:(c + 1) * 128], self.ident_b,
                        ["pbf", "cstb"], [kt])
            self.cp("act", self.pT[:, :, m * 128:(m + 1) * 128], pb[:, 0:256].rearrange("p (c i) -> p c i", c=2),
                    [kt], [("pT", m)])
        for dq in range(4):
            w = self.wbig.get()
            self.dma("pool", w.ap, self.wpg_d[l].rearrange("p (k n) -> p k n", k=8)[:, :, dq * 256:(dq + 1) * 256],
                     [], [w.key])
            for m in range(NB):
                pg, kg = self.psum()
                for k in range(8):
                    self.mm(pg[:, 0:256], self.xnT[:, k, m * 128:(m + 1) * 128], w.ap[:, k, :], k == 0, k == 7,
                            [("xnT", m), w.key], [kg])
                pj, kj = self.psum()
                for c in range(2):
                    self.mm(pj[:, 0:256], self.pT[:, c, m * 128:(m + 1) * 128], self.wpp[:, c, dq * 256:(dq + 1) * 256],
                            c == 0, c == 1, [("pT", m), "wpp"], [kj])
                sg = self.tmp.get()
                sgv = sg.ap[:, 0:256]
                self.act(sgv, pg[:, 0:256], AF.Sigmoid, [kg], [sg.key])
                self.tt("dve", sgv, sgv, pj[:, 0:256], ALU.mult, [sg.key, kj], [sg.key])
                hs = self.h[:, m, dq * 256:(dq + 1) * 256]
                hk = ("h", m, dq // 2, dq % 2)
                self.tt("dve", hs, hs, sgv, ALU.add, [sg.key, hk], [hk])

    def final(self, seq, t0):
        NB, T = self.NB, self.T
        ss = self.small[:, 0:NB]
        for m in range(NB):
            self.act(self.junk, self.h[:, m, :], AF.Square, [("h", m)], [("small", "ss", m), "junk"],
                     accum_out=self.small[:, m:m + 1])
        self.act(ss, ss, AF.Ln, [("small", "ss")], [("small", "ss")], bias=EPS, scale=1.0 / D)
        self.act(ss, ss, AF.Exp, [("small", "ss")], [("small", "ss")], scale=-0.5)
        for m in range(NB):
            self.stt("dve", self.h[:, m, :], self.h[:, m, :], self.small[:, m:m + 1], self.fing, ALU.mult, ALU.mult,
                     [("h", m), ("small", "ss"), "fing"], [("h", m)])
        for m in range(NB):
            op = self.dma("sp", self.out_d[seq, t0 + m * 128:t0 + (m + 1) * 128, :], self.h[:, m, :], [("h", m)], [])
            self.out_ops.append(op)

    def build(self):
        self.out_ops = []
        self.setup()
        for seq in range(self.nseq):
            self.zero_state()
            for t in range(self.NT):
                t0 = t * self.T
                for m in range(self.NB):
                    self.dma("sp", self.h[:, m, :], self.x_d[seq, t0 + m * 128:t0 + (m + 1) * 128, :], [], [("h", m)])
                for l in range(self.L):
                    if 'f1' in self.phases:
                        self.norm_T(l, 0)
                        self.ffn(l, 0)
                    if 'mix' in self.phases:
                        self.mixer(l, t)
                    if 'f2' in self.phases:
                        self.norm_T(l, 2)
                        self.ffn(l, 1)
                    if 'ple' in self.phases:
                        self.ple(l, seq, t0)
                self.final(seq, t0)
        self.s.analyze()
        self.s.emit(self.out_ops + self.dump_ops)
        return self.nc


_SHARED_CACHE = {}


def kernel(**inputs):
    inp = {k: np.asarray(v) for k, v in inputs.items()}
    shared = prep_shared(inp)
    n = 8
    b = Builder(nseq=2, ntiles=SEQ // 512, nlayers=DEPTH, T=512)
    nc = b.build()
    in_maps = []
    for c in range(n):
        m = dict(shared)
        m["x"] = np.ascontiguousarray(inp["x"][2 * c:2 * c + 2])
        m["p"] = np.ascontiguousarray(inp["p"][:, 2 * c:2 * c + 2])
        in_maps.append(m)
    res = run_bass_kernel_spmd(nc, in_maps, core_ids=list(range(n)))
    return np.concatenate([r["out"] for r in res.results], axis=0).astype(np.float32)
```

```python
import math
import numpy as np
import concourse.bass as bass
import concourse.mybir as mybir
from concourse.bass_utils import run_bass_kernel_spmd

F32 = mybir.dt.float32
BF16 = mybir.dt.bfloat16
AF = mybir.ActivationFunctionType
ALU = mybir.AluOpType
AX = mybir.AxisListType

COMPUTE = ("pe", "act", "dve", "pool")
SEM_WRAP = 30000
DMA_RING = 8

D = 1024
SEQ = 2048
DEPTH = 2
DFF = 2816
NFC = DFF // 128
PLE = 256
EPS = 1e-6
NEG = -30000.0


class Op:
    __slots__ = ("eng", "fn", "reads", "writes", "dma", "idx", "waits", "signal",
                 "sem", "semval", "prewait", "tag")

    def __init__(self, eng, fn, reads, writes, dma, tag=""):
        self.eng = eng
        self.fn = fn
        self.reads = reads
        self.writes = writes
        self.dma = dma
        self.waits = []
        self.signal = False
        self.sem = None
        self.semval = 0
        self.prewait = None
        self.tag = tag


class Sched:
    def __init__(self, nc, same_engine_sync=True):
        self.nc = nc
        self.ops = []
        self.same_engine_sync = same_engine_sync
        self.ent = {}
        self.children = {}

    def add(self, eng, fn, reads=(), writes=(), dma=False, tag=""):
        rk = [k if isinstance(k, tuple) else (k,) for k in reads]
        wk = [k if isinstance(k, tuple) else (k,) for k in writes]
        op = Op(eng, fn, rk, wk, dma, tag)
        op.idx = len(self.ops)
        self.ops.append(op)
        return op

    def _conflicts(self, k):
        out = []
        for n in range(1, len(k) + 1):
            e = self.ent.get(k[:n])
            if e is not None:
                out.append(e)
        for c in self.children.get(k, ()):
            out.append(self.ent[c])
        return out

    def _entry(self, k):
        e = self.ent.get(k)
        if e is None:
            e = [None, []]
            self.ent[k] = e
            for n in range(1, len(k)):
                self.children.setdefault(k[:n], set()).add(k)
        return e

    def analyze(self):
        ops = self.ops
        waited = {}
        waited_dma = {}
        dma_count = {}
        dma_hist = {}
        for op in ops:
            deps = set()
            for k in op.reads:
                for e in self._conflicts(k):
                    if e[0] is not None:
                        deps.add(e[0])
                    if k[0] == "ps":
                        for r_ in e[1]:
                            if ops[r_].eng != op.eng:
                                deps.add(r_)
            for k in op.writes:
                for e in self._conflicts(k):
                    if e[0] is not None:
                        deps.add(e[0])
                    deps.update(e[1])
            for k in op.reads:
                self._entry(k)[1].append(op.idx)
            for k in op.writes:
                e = self._entry(k)
                e[0] = op.idx
                e[1] = []
                for c in self.children.get(k, ()):
                    ce = self.ent[c]
                    ce[0] = op.idx
                    ce[1] = []
            deps.discard(op.idx)
            w = waited.setdefault(op.eng, {})
            wd = waited_dma.setdefault(op.eng, set())
            best = {}
            for d in deps:
                p = ops[d]
                if p.dma:
                    if d not in wd:
                        wd.add(d)
                        op.waits.append(d)
                    continue
                if p.eng == op.eng and not op.dma:
                    if op.eng == "pe" or not self.same_engine_sync:
                        continue
                if d > best.get(p.eng, -1):
                    best[p.eng] = d
            for e, d in best.items():
                if d > w.get(e, -1):
                    w[e] = d
                    op.waits.append(d)
                    ops[d].signal = True
            if op.dma:
                q = op.eng
                n = dma_count.get(q, 0)
                dma_count[q] = n + 1
                h = dma_hist.setdefault(q, [])
                if n >= DMA_RING:
                    op.prewait = h[n - DMA_RING]
                h.append(op.idx)
                op.signal = True
        return self

    def emit(self, final_wait_ops=()):
        nc = self.nc
        ops = self.ops
        sig_count = {}
        sems = {}
        dma_n = {}
        for op in ops:
            if not op.signal:
                continue
            if op.dma:
                q = op.eng
                n = dma_n.get(q, 0)
                dma_n[q] = n + 1
                ring = sems.setdefault(("dma", q), [])
                if len(ring) < DMA_RING:
                    ring.append(nc.alloc_semaphore(f"d_{q}_{len(ring)}"))
                op.sem = ring[n % DMA_RING]
                op.semval = 16 * (n // DMA_RING + 1)
            else:
                n = sig_count.get(op.eng, 0)
                sig_count[op.eng] = n + 1
                lst = sems.setdefault(op.eng, [])
                if n // SEM_WRAP >= len(lst):
                    lst.append(nc.alloc_semaphore(f"s_{op.eng}_{len(lst)}"))
                op.sem = lst[n // SEM_WRAP]
                op.semval = n % SEM_WRAP + 1
        by_eng = {}
        for op in ops:
            by_eng.setdefault(op.eng, []).append(op)
        self.stats = {e: len(v) for e, v in by_eng.items()}
        self.stats["signals"] = dict(sig_count)
        self.stats["nwaits"] = sum(len(o.waits) for o in ops)

        def run(engh, lst, finals):
            for op in lst:
                if op.prewait is not None:
                    p = ops[op.prewait]
                    engh.wait_ge(p.sem, p.semval)
                for d in op.waits:
                    p = ops[d]
                    engh.wait_ge(p.sem, p.semval)
                ins = op.fn(engh)
                if op.signal:
                    ins.then_inc(op.sem, 16 if op.dma else 1)
            for d in finals:
                p = ops[d]
                engh.wait_ge(p.sem, p.semval)

        finals = [o.idx for o in final_wait_ops]
        with nc.Block() as block:
            @block.sync
            def _(e):
                run(e, by_eng.get("sp", []), finals)

            @block.scalar
            def _(e):
                run(e, by_eng.get("act", []), [])

            @block.vector
            def _(e):
                run(e, by_eng.get("dve", []), [])

            @block.gpsimd
            def _(e):
                run(e, by_eng.get("pool", []), [])

            @block.tensor
            def _(e):
                run(e, by_eng.get("pe", []), [])


class Ring:
    def __init__(self, nc, name, shape, dtype, n):
        self.name = name
        self.bufs = [nc.alloc_sbuf_tensor(f"rg_{name}{i}", list(shape), dtype).ap() for i in range(n)]
        self.gen = [0] * n
        self.n = n
        self.i = 0

    def get(self):
        i = self.i
        self.i = (i + 1) % self.n
        self.gen[i] += 1
        return Tmp(self, i, self.gen[i])


class ViewRing:
    def __init__(self, name, bufs, keys):
        self.name = name
        self.bufs = bufs
        self.keys = keys
        self.n = len(bufs)
        self.gen = [0] * self.n
        self.i = 0

    def get(self):
        i = self.i
        self.i = (i + 1) % self.n
        self.gen[i] += 1
        t = Tmp(self, i, self.gen[i])
        t.key = self.keys[i]
        return t


class Ctx:
    def __init__(self, tmp, tmpb, pad, banks):
        self.tmp = tmp
        self.tmpb = tmpb
        self.pad = pad
        self.banks = banks
        self.i = 0


class Tmp:
    def __init__(self, ring, i, gen):
        self.ring = ring
        self.i = i
        self.g = gen
        self.key = (ring.name, i)

    @property
    def ap(self):
        assert self.ring.gen[self.i] == self.g, f"stale tmp {self.ring.name}[{self.i}]"
        return self.ring.bufs[self.i]


IN_OFF = dict(lru_x=0, lru_g=256, att_q=512, att_k=1024, att_v=1152, dn_q=1280, dn_k=1536,
              dn_v=1792, dn_z=2048, dn_b=2304, dn_a=2308)
NF = 15
NTM = 392


def _fcols():
    cols = []
    for c in range(2):
        cols.append(np.arange(IN_OFF["lru_x"] + c * 128, IN_OFF["lru_x"] + (c + 1) * 128))
    for c in range(2):
        cols.append(np.arange(IN_OFF["lru_g"] + c * 128, IN_OFF["lru_g"] + (c + 1) * 128))
    for g in range(4):
        a = np.arange(IN_OFF["att_q"] + g * 64, IN_OFF["att_q"] + (g + 1) * 64)
        b = np.arange(IN_OFF["att_q"] + (4 + g) * 64, IN_OFF["att_q"] + (5 + g) * 64)
        cols.append(np.concatenate([a, b]))
    cols.append(np.arange(IN_OFF["att_k"], IN_OFF["att_k"] + 128))
    for nm in ("dn_q", "dn_k", "dn_v"):
        for c in range(2):
            cols.append(np.arange(IN_OFF[nm] + c * 128, IN_OFF[nm] + (c + 1) * 128))
    return cols


def _tmcols():
    return np.concatenate([np.arange(IN_OFF["att_v"], IN_OFF["att_v"] + 128),
                           np.arange(IN_OFF["dn_z"], IN_OFF["dn_z"] + 256),
                           np.arange(IN_OFF["dn_b"], IN_OFF["dn_b"] + 4),
                           np.arange(IN_OFF["dn_a"], IN_OFF["dn_a"] + 4)])


def _mix_rows():
    rows = [np.arange(0, 128), np.arange(128, 256)]
    for g in range(4):
        a = np.arange(256 + g * 64, 256 + (g + 1) * 64)
        b = np.arange(256 + (4 + g) * 64, 256 + (5 + g) * 64)
        rows.append(np.concatenate([a, b]))
    rows.append(np.arange(768, 896))
    rows.append(np.arange(896, 1024))
    return np.concatenate(rows)


def _rel_bucket(dist):
    max_exact = 16
    d = np.maximum(dist, 1).astype(np.float32)
    large = max_exact + (np.log(d / np.float32(max_exact)) / np.float32(math.log(128 / max_exact))
                         * np.float32(32 - max_exact)).astype(np.int32)
    large = np.minimum(large, 31)
    return np.where(dist < max_exact, dist, large)


def _kchunk(w):
    return np.ascontiguousarray(w.reshape(8, 128, -1).transpose(1, 0, 2))


def prep_shared(inp):
    f = np.float32
    out = {}
    L = DEPTH
    wgu = np.empty((2, L, NFC, 128, 2, 8, 128), f)
    wd = np.empty((2, L, NFC, 128, 1024), f)
    for fi, nm in enumerate(("ffn1", "ffn2")):
        for l in range(L):
            g = inp[nm + "_w_gate"][l].reshape(8, 128, NFC, 128)
            u = inp[nm + "_w_up"][l].reshape(8, 128, NFC, 128)
            wgu[fi, l, :, :, 0] = g.transpose(2, 1, 0, 3)
            wgu[fi, l, :, :, 1] = u.transpose(2, 1, 0, 3)
            wd[fi, l] = inp[nm + "_w_down"][l].reshape(NFC, 128, 1024)
    out["wgu"] = wgu.reshape(2 * L * NFC, 128, 2048)
    out["wd"] = wd.reshape(2 * L * NFC, 128, 1024)
    fc = _fcols()
    tm = _tmcols()
    winF = np.empty((L, NF, 128, 8, 128), f)
    winT = np.empty((L, 128, 8, NTM), f)
    for l in range(L):
        w = inp["w_in"][l]
        for i, c in enumerate(fc):
            winF[l, i] = _kchunk(w[:, c])
        winT[l] = _kchunk(w[:, tm])
    out["winF"] = winF.reshape(L * NF, 128, 1024)
    out["winT"] = winT.reshape(L, 128, 8 * NTM)
    rows = _mix_rows()
    out["wout"] = np.stack([_kchunk(inp["w_out"][l][rows]) for l in range(L)]).reshape(L, 128, 8 * 1024)
    out["wpg"] = np.stack([_kchunk(inp["ple_w_gate"][l]) for l in range(L)]).reshape(L, 128, 8 * 1024)
    out["wpp"] = np.stack([inp["ple_w_proj"][l].reshape(2, 128, 1024).transpose(1, 0, 2)
                           for l in range(L)]).reshape(L, 128, 2 * 1024).copy()
    pp = np.zeros((L, 128, 72), f)
    for l in range(L):
        for j, nm in enumerate(("ffn1_norm", "mix_norm", "ffn2_norm", "ple_norm")):
            pp[l, :, j * 8:(j + 1) * 8] = inp[nm][l].reshape(8, 128).T
        pp[l, :, 32:40] = inp["lru_conv_w"][l].reshape(4, 2, 128).transpose(2, 1, 0).reshape(128, 8)
        pp[l, :, 40:42] = inp["lru_conv_b"][l].reshape(2, 128).T
        pp[l, :, 42:44] = inp["lru_b_a"][l].reshape(2, 128).T
        pp[l, :, 44:46] = inp["lru_b_x"][l].reshape(2, 128).T
        pp[l, :, 46:48] = inp["lru_lambda"][l].reshape(2, 128).T
        pp[l, :, 48:72] = inp["dn_conv_w"][l].reshape(4, 6, 128).transpose(2, 1, 0).reshape(128, 24)
    out["pp"] = pp
    rp = np.zeros((L, 128, 80), f)
    for l in range(L):
        rp[l, :, 0:8] = inp["attn_sinks"][l][None, :]
        rp[l, :, 8:12] = inp["dn_a_log"][l][None, :]
        rp[l, :, 12:16] = inp["dn_dt_bias"][l][None, :]
        rp[l, :, 16:80] = inp["dn_norm"][l][None, :]
    out["rp"] = rp
    out["fing"] = np.ascontiguousarray(np.broadcast_to(inp["final_norm"][None, :], (128, 1024))).astype(f)
    wab = np.zeros((L, 128, 2, 2, 128), f)
    for l in range(L):
        for ai, nm in enumerate(("lru_w_a", "lru_w_x")):
            for c in range(2):
                for hh in range(2):
                    wab[l, hh * 64:(hh + 1) * 64, ai, c, hh * 64:(hh + 1) * 64] = inp[nm][l, 2 * c + hh]
    out["wab"] = wab.reshape(L, 128, 512)
    rb = inp["rel_bias"]
    j = np.arange(128)[:, None]
    i = np.arange(128)[None, :]
    bias = np.zeros((128, 2, 2, 4, 128), f)
    for kb in range(2):
        dist = 128 + i - (kb * 128 + j)
        valid = (dist >= 0) & (dist < 128)
        bk = _rel_bucket(np.maximum(dist, 0))
        for kh in range(2):
            for g in range(4):
                gathered = rb[bk, kh * 4 + g]
                bias[:, kh, kb, g, :] = np.where(valid, gathered, f(NEG))
    out["abias"] = bias.reshape(128, 2 * 2 * 512)
    cst = np.zeros((128, 640), f)
    cst[:, 0:128] = np.eye(128, dtype=f)
    cst[:, 128:256] = np.triu(np.ones((128, 128), f))
    cst[:, 256:384] = np.tril(np.ones((128, 128), f), -1)
    cst[:, 384:512] = 1.0
    bo = np.zeros((128, 128), f)
    bo[0:64, 0:64] = 1.0
    bo[64:128, 64:128] = 1.0
    cst[:, 512:640] = bo
    out["cst"] = cst
    return out


class Builder:
    def __init__(self, nseq=2, ntiles=4, nlayers=2, T=512, dump=None, tdt=F32, phases=None, mix=None):
        self.nseq = nseq
        self.NT = ntiles
        self.L = nlayers
        self.T = T
        self.NB = T // 128
        self.tdt = tdt
        self.dump = dump or []
        self.blk_stop = 99
        self.ilv = True
        self.ilv_w = (1, 2, 2)
        self.mix_id = 0
        self.phases = phases or ('f1', 'mix', 'f2', 'ple')
        self.mix = mix or ('tm', 'sc', 'lru', 'att', 'pre', 'blk', 'out')
        self.dump_ops = []
        nc = bass.Bass("TRN2", target_bir_lowering=False)
        self.nc = nc
        self.s = Sched(nc)
        self._dram()
        self._sbuf()

    def _dram(self):
        nc = self.nc
        L = DEPTH

        def di(name, shape):
            return nc.dram_tensor(name, list(shape), F32, kind="ExternalInput").ap()
        self.x_d = di("x", (self.nseq, SEQ, D))
        self.p_d = di("p", (L, self.nseq, SEQ, PLE))
        self.wgu_d = di("wgu", (2 * L * NFC, 128, 2048))
        self.wd_d = di("wd", (2 * L * NFC, 128, 1024))
        self.winF_d = di("winF", (L * NF, 128, 1024))
        self.winT_d = di("winT", (L, 128, 8 * NTM))
        self.wout_d = di("wout", (L, 128, 8192))
        self.wpg_d = di("wpg", (L, 128, 8192))
        self.wpp_d = di("wpp", (L, 128, 2048))
        self.pp_d = di("pp", (L, 128, 72))
        self.rp_d = di("rp", (L, 128, 80))
        self.fing_d = di("fing", (128, 1024))
        self.wab_d = di("wab", (L, 128, 512))
        self.abias_d = di("abias", (128, 2048))
        self.cst_d = di("cst", (128, 640))
        self.out_d = nc.dram_tensor("out", [self.nseq, SEQ, D], F32, kind="ExternalOutput").ap()

    def _sb(self, name, shape, dtype):
        return self.nc.alloc_sbuf_tensor("sb_" + name, list(shape), dtype).ap()

    def _sbuf(self):
        nc = self.nc
        NB, T, L = self.NB, self.T, DEPTH
        self.ps = [nc.alloc_psum_tensor(f"ps{i}", [128, 512], F32).ap() for i in range(8)]
        self.cst = self._sb("cst", (128, 640), F32)
        self.ident_f = self.cst[:, 0:128]
        self.U_f = self.cst[:, 128:256]
        self.SL_f = self.cst[:, 256:384]
        self.ones_f = self.cst[:, 384:512]
        self.cstb = self._sb("cstb", (128, 384), BF16)
        self.ident_b = self.cstb[:, 0:128]
        self.ones_b = self.cstb[:, 128:256]
        self.bones_b = self.cstb[:, 256:384]
        self.abias = self._sb("abias", (128, 2, 2, 512), F32)
        self.pp = self._sb("pp", (128, L, 72), F32)
        self.rp = self._sb("rp", (128, L, 80), F32)
        self.fing = self._sb("fing", (128, 1024), F32)
        self.wab = self._sb("wab", (128, L, 2, 2, 128), BF16)
        self.der = self._sb("der", (128, L, 32), F32)
        self.halo_lru = self._sb("halo_lru", (128, L, 2, 3), F32)
        self.halo_dn = self._sb("halo_dn", (128, L, 6, 3), F32)
        self.kTpad = self._sb("kTpad", (128, L, 128 + T), BF16)
        self.vpad = self._sb("vpad", (128, L, NB + 1, 128), BF16)
        self.S = self._sb("S", (128, L, 2, 64), F32)
        self.Sb = self._sb("Sb", (128, L, 2, 64), BF16)
        self.lru_h = self._sb("lru_h", (128, L, 2), F32)
        self.h = self._sb("h", (128, NB, D), F32)
        self.xnT = self._sb("xnT", (128, 8, T), BF16)
        self.yT = self._sb("yT", (128, 8, T), BF16)
        self.GS = 4
        self.hT = self._sb("hT", (128, self.GS, T), BF16)
        self.wgu = Ring(nc, "wgu", (128, 2, 8, 128), BF16, 3)
        self.wd = Ring(nc, "wd", (128, self.GS, 1024), BF16, 2)
        self.wF = Ring(nc, "wF", (128, 8, 128), BF16, 3)
        self.wTM = self._sb("wTM", (128, 8, NTM), BF16)
        self.wbig = Ring(nc, "wbig", (128, 8, 256), BF16, 2)
        self.wpp = self._sb("wpp", (128, 2, 1024), BF16)
        RXf = Ring(nc, "tmpx", (128, 512), F32, 9)
        RXb = Ring(nc, "tmpbx", (128, 512), BF16, 4)
        RYf = Ring(nc, "tmpy", (128, 512), F32, 7)
        RYb = Ring(nc, "tmpby", (128, 512), BF16, 12)
        padX = Ring(nc, "padx", (128, 3 + T), F32, 2)
        padY = Ring(nc, "pady", (128, 3 + T), F32, 2)
        self.cx_main = Ctx(RXf, RXb, padX, list(range(8)))
        self.cxX = Ctx(RXf, RXb, padX, [0, 1])
        self.cxY = Ctx(RYf, RYb, padY, [2, 3, 4])
        self.cx = self.cx_main
        self.xs = Ring(nc, "xs", (128, 1024), BF16, 2)
        zf_b, zf_k = [], []
        for j in range(7):
            zf_b.append(self.wd.bufs[j // 4][:, j % 4, :].bitcast(F32))
            zf_k.append(("wd", j // 4, j % 4))
        zb_b, zb_k = [], []
        for j in range(12):
            sl = j % 4
            zb_b.append(self.wgu.bufs[j // 4][:, sl // 2, (sl % 2) * 4:(sl % 2) * 4 + 4, :].rearrange("p a b -> p (a b)"))
            zb_k.append(("wgu", j // 4, sl))
        self.cxZ = Ctx(ViewRing("zf", zf_b, zf_k), ViewRing("zb", zb_b, zb_k), None, [5, 6, 7])
        self.small = self._sb("small", (128, 64), F32)
        self.junk = self._sb("junk", (128, 1024), BF16)
        self.q_sb = self._sb("q_sb", (128, 4, T), BF16)
        self.qT_dn = self._sb("qT_dn", (128, 2, T), BF16)
        self.kT_dn = self._sb("kT_dn", (128, 2, T), BF16)
        self.vT_dn = self._sb("vT_dn", (128, 2, T), BF16)
        self.kbg = self._sb("kbg", (128, NB, 256), BF16)
        self.kdec = self._sb("kdec", (128, NB, 256), BF16)
        self.vb = self._sb("vb", (128, NB, 256), BF16)
        self.gz = self._sb("gz", (128, NB, 256), F32)
        self.ba = self._sb("ba", (128, NB, 8), F32)
        self.sc = self._sb("sc", (128, 12, NB, 4), F32)
        self.EG = self._sb("EG", (128, 4, 128), F32)
        self.attnT = self._sb("attnT", (128, 4, 128), BF16)
        self.Ttb = self._sb("Ttb", (128, 4, 128), BF16)
        self.u_sb = self._sb("u_sb", (128, 256), F32)
        self.wT = self._sb("wT", (128, 2, 128), BF16)
        self.qgT = self._sb("qgT", (128, 2, 128), BF16)
        self.bsY = dict(EG=self.EG, EGk="EG", attnT=self.attnT, attnTk="attnT", Ttb=self.Ttb, Ttbk="Ttb",
                        u_sb=self.u_sb, u_sbk="u_sb", wT=self.wT, wTk=("wT",), qgT=self.qgT, qgTk=("qgT",))
        h4 = lambda ap: ap.rearrange("p (h i) -> p h i", h=4)
        self.bsZ = dict(EG=h4(self.wd.bufs[1][:, 3, :].bitcast(F32)), EGk=("wd", 1, 3),
                        attnT=h4(self.hT[:, 0, :]), attnTk=("hT", 0), Ttb=h4(self.hT[:, 1, :]), Ttbk=("hT", 1),
                        u_sb=self.hT[:, 2, :].bitcast(F32), u_sbk=("hT", 2),
                        wT=self.hT[:, 3, 0:256].rearrange("p (c i) -> p c i", c=2), wTk=("hT", 3, 0),
                        qgT=self.hT[:, 3, 256:512].rearrange("p (c i) -> p c i", c=2), qgTk=("hT", 3, 1))
        self.pbf = self._sb("pbf", (128, NB, PLE), BF16)
        self.pT = self._sb("pT", (128, 2, T), BF16)

    @property
    def tmp(self):
        return self.cx.tmp

    @property
    def tmpb(self):
        return self.cx.tmpb

    @property
    def pad(self):
        return self.cx.pad

    def psum(self):
        cx = self.cx
        i = cx.banks[cx.i]
        cx.i = (cx.i + 1) % len(cx.banks)
        return self.ps[i], ("ps", i)

    def interleave(self, streams, weights=None):
        fired = set()
        active = [[cx, g, (weights[i] if weights else 1), None] for i, (cx, g) in enumerate(streams)]
        while active:
            progressed = False
            for item in list(active):
                cx, g, wgt, _ = item
                self.cx = cx
                for _n in range(wgt):
                    if item[3] is not None:
                        if item[3] in fired:
                            item[3] = None
                        else:
                            break
                    try:
                        v = next(g)
                    except StopIteration:
                        active.remove(item)
                        progressed = True
                        break
                    progressed = True
                    if isinstance(v, tuple):
                        if v[0] == "set":
                            fired.add(v[1])
                        elif v[0] == "wait" and v[1] not in fired:
                            item[3] = v[1]
                            break
            assert progressed, "interleave deadlock"
        self.cx = self.cx_main

    def mm(self, out, lhsT, rhs, start, stop, r, w):
        return self.s.add("pe", lambda e: e.matmul(out, lhsT=lhsT, rhs=rhs, start=start, stop=stop,
                                                   skip_group_check=True), r, w)

    def tr(self, out, in_, ident, r, w):
        return self.s.add("pe", lambda e: e.transpose(out=out, in_=in_, identity=ident), r, w)

    def act(self, out, in_, func, r, w, bias=None, scale=None, accum_out=None):
        kw = {}
        if bias is not None:
            kw["bias"] = bias
        if scale is not None:
            kw["scale"] = scale
        if accum_out is not None:
            kw["accum_out"] = accum_out
        return self.s.add("act", lambda e: e.activation(out=out, in_=in_, func=func, **kw), r, w)

    def tt(self, eng, out, in0, in1, op, r, w):
        return self.s.add(eng, lambda e: e.tensor_tensor(out=out, in0=in0, in1=in1, op=op), r, w)

    def ts(self, eng, out, in0, s1, s2, op0, op1, r, w):
        if s2 is None:
            return self.s.add(eng, lambda e: e.tensor_scalar(out=out, in0=in0, scalar1=s1, scalar2=None, op0=op0), r, w)
        return self.s.add(eng, lambda e: e.tensor_scalar(out=out, in0=in0, scalar1=s1, scalar2=s2,
                                                         op0=op0, op1=op1), r, w)

    def stt(self, eng, out, in0, scalar, in1, op0, op1, r, w):
        return self.s.add(eng, lambda e: e.scalar_tensor_tensor(out=out, in0=in0, scalar=scalar, in1=in1,
                                                                op0=op0, op1=op1), r, w)

    def sig3(self, out, in_, r, w, nbias=None, nscale=-1.0):
        self.act(out, in_, AF.Exp, r, w, bias=nbias, scale=nscale)
        yield
        self.act(out, out, AF.Ln, w, w, bias=1.0)
        yield
        self.act(out, out, AF.Exp, w, w, scale=-1.0)

    def sig3_now(self, *a, **kw):
        for _ in self.sig3(*a, **kw):
            pass

    def cp(self, eng, out, in_, r, w):
        if eng == "act":
            return self.s.add("act", lambda e: e.copy(out=out, in_=in_), r, w)
        return self.s.add(eng, lambda e: e.tensor_copy(out=out, in_=in_), r, w)

    def dma(self, q, out, in_, r, w):
        return self.s.add(q, lambda e: e.dma_start(out=out, in_=in_), r, w, dma=True)

    def memset(self, eng, ap, val, w):
        return self.s.add(eng, lambda e: e.memset(ap, val), [], w)

    def dbg(self, name, ap, key, dtype=F32):
        if name not in self.dump:
            return
        shape = list(ap.shape)
        d = self.nc.dram_tensor("dbg_" + name, shape, dtype, kind="ExternalOutput").ap()
        op = self.dma("sp", d, ap, [key], [])
        self.dump_ops.append(op)

    def setup(self):
        L = DEPTH
        self.dma("sp", self.cst, self.cst_d, [], ["cst"])
        self.dma("sp", self.abias, self.abias_d.rearrange("p (a b n) -> p a b n", a=2, b=2), [], ["abias"])
        self.dma("sp", self.pp, self.pp_d.rearrange("l p n -> p l n"), [], ["pp"])
        self.dma("sp", self.rp, self.rp_d.rearrange("l p n -> p l n"), [], ["rp"])
        self.dma("sp", self.fing, self.fing_d, [], ["fing"])
        self.dma("pool", self.wab, self.wab_d.rearrange("l p (a c n) -> p l a c n", a=2, c=2), [], ["wab"])
        self.cp("dve", self.ident_b, self.ident_f, ["cst"], ["cstb"])
        self.cp("dve", self.ones_b, self.ones_f, ["cst"], ["cstb"])
        self.cp("dve", self.bones_b, self.cst[:, 512:640], ["cst"], ["cstb"])
        for l in range(L):
            der = self.der[:, l, :]
            lam = self.pp[:, l, 46:48]
            self.act(der[:, 16:18], lam, AF.Exp, ["pp"], [("der", l)], scale=-1.0)
            self.act(der[:, 16:18], der[:, 16:18], AF.Ln, [("der", l)], [("der", l)], bias=1.0)
            self.ts("dve", der[:, 0:2], der[:, 16:18], -8.0, None, ALU.mult, None, [("der", l)], [("der", l)])
            self.ts("dve", der[:, 2:4], der[:, 16:18], -16.0, None, ALU.mult, None, [("der", l)], [("der", l)])
            self.act(der[:, 4:8], self.rp[:, l, 8:12], AF.Exp, ["rp"], [("der", l)])
            self.ts("dve", der[:, 4:8], der[:, 4:8], -1.0, None, ALU.mult, None, [("der", l)], [("der", l)])
            self.ts("dve", der[:, 24:26], self.pp[:, l, 42:44], -1.0, None, ALU.mult, None, ["pp"], [("der", l)])
            self.ts("dve", der[:, 26:28], self.pp[:, l, 44:46], -1.0, None, ALU.mult, None, ["pp"], [("der", l)])
            self.act(der[:, 8:16], self.rp[:, l, 0:8], AF.Exp, ["rp"], [("der", l)])

    def zero_state(self):
        for l in range(self.L):
            self.memset("dve", self.halo_lru[:, l], 0.0, [("halo_lru", l)])
            self.memset("dve", self.halo_dn[:, l], 0.0, [("halo_dn", l)])
            self.memset("dve", self.kTpad[:, l, 0:128], 0.0, [("kTpad", l, "h")])
            self.memset("dve", self.vpad[:, l, 0, :], 0.0, [("vpad", l, "h")])
            self.memset("dve", self.S[:, l], 0.0, [("S", l)])
            self.memset("dve", self.Sb[:, l], 0.0, [("Sb", l)])
            self.memset("dve", self.lru_h[:, l], 0.0, [("lru_h", l)])

    def norm_T(self, l, j):
        NB = self.NB
        ss = self.small[:, 0:NB]
        for m in range(NB):
            self.act(self.junk, self.h[:, m, :], AF.Square, [("h", m)], [("small", "ss", m), "junk"],
                     accum_out=self.small[:, m:m + 1])
        self.act(ss, ss, AF.Ln, [("small", "ss")], [("small", "ss")], bias=EPS, scale=1.0 / D)
        self.act(ss, ss, AF.Exp, [("small", "ss")], [("small", "ss")], scale=-0.5)
        gam = self.pp[:, l, j * 8:(j + 1) * 8].unsqueeze(2).to_broadcast([128, 8, 128])
        for m in range(NB):
            xs = self.xs.get()
            self.act(xs.ap, self.h[:, m, :], AF.Copy, [("h", m), ("small", "ss")], [xs.key],
                     scale=self.small[:, m:m + 1])
            pt, pk = self.psum()
            pv = pt.bitcast(BF16).rearrange("p (c t) -> p c t", c=8)
            for c in range(8):
                self.tr(pv[:, c, :], xs.ap[:, c * 128:(c + 1) * 128], self.ident_b, [xs.key, "cstb"], [pk])
            self.tt("dve", self.xnT[:, :, m * 128:(m + 1) * 128], pv, gam, ALU.mult, [pk, "pp"], [("xnT", m)])

    def ffn(self, l, fi):
        NB, T = self.NB, self.T
        base = (fi * DEPTH + l) * NFC
        groups = []
        c0 = 0
        while c0 < NFC:
            n = min(self.GS, NFC - c0)
            groups.append(list(range(c0, c0 + n)))
            c0 += n
        for grp in groups:
            wd = self.wd.get()
            n = len(grp)
            self.dma("pool", wd.ap[:, 0:n, :],
                     self.wd_d[base + grp[0]:base + grp[0] + n].rearrange("c p n -> p c n"), [], [wd.key])
            for jj, c in enumerate(grp):
                wg = self.wgu.get()
                self.dma("pool", wg.ap, self.wgu_d[base + c].rearrange("p (a k n) -> p a k n", a=2, k=8),
                         [], [wg.key])
                pg, kg = self.psum()
                pu, ku = self.psum()
                for k in range(8):
                    self.mm(pg, wg.ap[:, 0, k, :], self.xnT[:, k, :], k == 0, k == 7, [wg.key, ("xnT",)], [kg])
                for k in range(8):
                    self.mm(pu, wg.ap[:, 1, k, :], self.xnT[:, k, :], k == 0, k == 7, [wg.key, ("xnT",)], [ku])
                sg = self.tmp.get()
                self.act(sg.ap, pg, AF.Silu, [kg], [sg.key])
                self.tt("dve", self.hT[:, jj, :], sg.ap, pu, ALU.mult, [sg.key, ku], [("hT", jj)])
            for m in range(NB):
                for dh in range(2):
                    po, ko = self.psum()
                    for jj in range(n):
                        self.mm(po, self.hT[:, jj, m * 128:(m + 1) * 128], wd.ap[:, jj, dh * 512:(dh + 1) * 512],
                                jj == 0, jj == n - 1, [("hT", jj), wd.key], [ko])
                    hs = self.h[:, m, dh * 512:(dh + 1) * 512]
                    self.stt("dve", hs, po, 0.5, hs, ALU.mult, ALU.add, [ko, ("h", m, dh)], [("h", m, dh)])

    def inproj_F(self, l, i):
        wf = self.wF.get()
        self.dma("pool", wf.ap, self.winF_d[l * NF + i].rearrange("p (k n) -> p k n", k=8), [], [wf.key])
        pf, kf = self.psum()
        for k in range(8):
            self.mm(pf, wf.ap[:, k, :], self.xnT[:, k, :], k == 0, k == 7, [wf.key, ("xnT",)], [kf])
        return pf, kf

    def conv(self, eng, pad, wcols, bias, l):
        T = self.T
        o = self.tmp.get()
        w = lambda k: wcols[:, k:k + 1]
        self.ts(eng, o.ap, pad.ap[:, 0:T], w(0), 0.0 if bias is None else bias, ALU.mult, ALU.add,
                [pad.key, "pp"], [o.key])
        for k in range(1, 4):
            self.stt(eng, o.ap, pad.ap[:, k:k + T], w(k), o.ap, ALU.mult, ALU.add, [pad.key, "pp", o.key], [o.key])
        return o

    def lru(self, l, c):
        T = self.T
        pp = self.pp[:, l, :]
        der = self.der[:, l, :]
        pad = self.pad.get()
        yield
        self.cp("act", pad.ap[:, 0:3], self.halo_lru[:, l, c, :], [("halo_lru", l, c)], [pad.key])
        px, kx = self.inproj_F(l, c)
        yield
        self.cp("act", pad.ap[:, 3:3 + T], px, [kx], [pad.key])
        pgate, kgate = self.inproj_F(l, 2 + c)
        gsb = self.tmp.get()
        yield
        self.cp("act", gsb.ap, pgate, [kgate], [gsb.key])
        xr = self.conv("dve", pad, pp[:, 32 + c * 4:36 + c * 4], pp[:, 40 + c:41 + c], l)
        yield
        self.cp("act", self.halo_lru[:, l, c, :], pad.ap[:, T:T + 3], [pad.key], [("halo_lru", l, c)])
        xrb = self.tmpb.get()
        yield
        self.cp("act", xrb.ap, xr.ap, [xr.key], [xrb.key])
        pr, kr = self.psum()
        self.mm(pr, self.wab[:, l, 0, c, :], xrb.ap, True, True, ["wab", xrb.key], [kr])
        pi, ki = self.psum()
        self.mm(pi, self.wab[:, l, 1, c, :], xrb.ap, True, True, ["wab", xrb.key], [ki])
        r = self.tmp.get()
        yield
        yield from self.sig3(r.ap, pr, [kr, ("der", l)], [r.key], nbias=der[:, 24 + c:25 + c])
        ig = self.tmp.get()
        yield
        yield from self.sig3(ig.ap, pi, [ki, ("der", l)], [ig.key], nbias=der[:, 26 + c:27 + c])
        a = self.tmp.get()
        yield
        self.act(a.ap, r.ap, AF.Exp, [r.key, ("der", l)], [a.key], scale=der[:, c:c + 1])
        a2 = self.tmp.get()
        yield
        self.act(a2.ap, r.ap, AF.Exp, [r.key, ("der", l)], [a2.key], scale=der[:, 2 + c:3 + c])
        yield
        self.act(a2.ap, a2.ap, AF.Ln, [a2.key], [a2.key], bias=1.0, scale=-1.0)
        yield
        self.act(a2.ap, a2.ap, AF.Exp, [a2.key], [a2.key], scale=0.5)
        yield
        self.tt("dve", ig.ap, ig.ap, xr.ap, ALU.mult, [ig.key, xr.key], [ig.key])
        yield
        self.tt("dve", ig.ap, ig.ap, a2.ap, ALU.mult, [ig.key, a2.key], [ig.key])
        hs = self.tmp.get()
        st = self.lru_h[:, l, c:c + 1]
        hs_ap, a_ap, ig_ap = hs.ap, a.ap, ig.ap
        yield
        self.s.add("dve", lambda e: e.tensor_tensor_scan(out=hs_ap, data0=a_ap, data1=ig_ap, initial=st,
                                                         op0=ALU.mult, op1=ALU.add),
                   [a.key, ig.key, ("lru_h", l, c)], [hs.key])
        yield
        self.cp("dve", st, hs.ap[:, T - 1:T], [hs.key], [("lru_h", l, c)])
        sq = self.tmp.get()
        yield
        self.act(sq.ap, gsb.ap, AF.Square, [gsb.key], [sq.key])
        yield
        self.ts("dve", sq.ap, sq.ap, 0.044715, 1.0, ALU.mult, ALU.add, [sq.key], [sq.key])
        yield
        self.tt("dve", sq.ap, sq.ap, gsb.ap, ALU.mult, [sq.key, gsb.key], [sq.key])
        yield
        yield from self.sig3(sq.ap, sq.ap, [sq.key], [sq.key], nscale=-2.0 * math.sqrt(2.0 / math.pi))
        yield
        self.tt("dve", sq.ap, sq.ap, gsb.ap, ALU.mult, [sq.key, gsb.key], [sq.key])
        yield
        self.tt("dve", self.yT[:, c, :], sq.ap, hs.ap, ALU.mult, [sq.key, hs.key], [("yT", "l", c)])

    def inproj_TM(self, l):
        NB = self.NB
        self.dma("pool", self.wTM, self.winT_d[l].rearrange("p (k n) -> p k n", k=8), [], ["wTM"])
        for m in range(NB):
            pt, kt = self.psum()
            for k in range(8):
                self.mm(pt[:, 0:NTM], self.xnT[:, k, m * 128:(m + 1) * 128], self.wTM[:, k, :], k == 0, k == 7,
                        [("xnT", m), "wTM"], [kt])
            self.cp("act", self.vpad[:, l, 1 + m, :], pt[:, 0:128], [kt], [("vpad", l, "b", m)])
            self.sig3_now(self.gz[:, m, :], pt[:, 128:384], [kt], [("gz", m)])
            self.tt("dve", self.gz[:, m, :], self.gz[:, m, :], pt[:, 128:384], ALU.mult, [kt, ("gz", m)], [("gz", m)])
            self.cp("act", self.ba[:, m, :], pt[:, 384:392], [kt], [("ba", m)])
        dnn = self.rp[:, l, 16:80].unsqueeze(1).unsqueeze(1).to_broadcast([128, NB, 4, 64])
        gzv = self.gz.rearrange("p m (h d) -> p m h d", h=4)
        self.tt("dve", gzv, gzv, dnn, ALU.mult, [("gz",), "rp"], [("gz",)])

    def attention(self, l, gtile):
        NB, T = self.NB, self.T
        for i in range(4):
            pq, kq = self.inproj_F(l, 4 + i)
            yield
            self.s.add("act", (lambda o, p: lambda e: e.mul(out=o, in_=p, mul=0.125))(self.q_sb[:, i, :], pq),
                       [kq], [("q_sb", i)])
        pk_, kk = self.inproj_F(l, 8)
        yield
        self.cp("act", self.kTpad[:, l, 128:128 + T], pk_, [kk], [("kTpad", l, "b")])
        esink = self.der[:, l, 8:16]
        for b in range(NB):
            first = (gtile * NB + b == 0)
            for kh in range(2):
                lo, hi = kh * 64, kh * 64 + 64
                es = []
                for kb in ((1,) if first else (0, 1)):
                    col0 = (b + kb) * 128
                    pS, kS = self.psum()
                    self.mm(pS.rearrange("p (g i) -> p g i", g=4), self.kTpad[lo:hi, l, col0:col0 + 128],
                            self.q_sb[lo:hi, :, b * 128:(b + 1) * 128], True, False,
                            [("kTpad", l), ("q_sb",)], [kS])
                    self.mm(pS, self.ident_f, self.abias[:, kh, kb, :], False, True, ["cst", "abias"], [kS])
                    e = self.tmpb.get()
                    yield
                    self.act(e.ap, pS, AF.Exp, [kS], [e.key])
                    es.append((kb, e))
                pden, kden = self.psum()
                for n_, (kb, e) in enumerate(es):
                    self.mm(pden, self.ones_b, e.ap, n_ == 0, n_ == len(es) - 1, ["cstb", e.key], [kden])
                po, ko = self.psum()
                for n_, (kb, e) in enumerate(es):
                    self.mm(po, self.vpad[:, l, b + kb, :], e.ap, n_ == 0, n_ == len(es) - 1,
                            [("vpad", l), e.key], [ko])
                rd = self.tmp.get()
                rdv = rd.ap.rearrange("p (g i) -> p g i", g=4)[lo:hi]
                pdv = pden.rearrange("p (g i) -> p g i", g=4)[lo:hi]
                pov = po.rearrange("p (g i) -> p g i", g=4)[lo:hi]
                esb = esink[lo:hi, kh * 4:(kh + 1) * 4].unsqueeze(2).to_broadcast([64, 4, 128])
                yield
                self.tt("dve", rdv, pdv, esb, ALU.add, [kden, ("der", l)], [rd.key])
                yield
                self.act(rdv, rdv, AF.Ln, [rd.key], [rd.key])
                yield
                self.act(rdv, rdv, AF.Exp, [rd.key], [rd.key], scale=-1.0)
                yield
                self.tt("dve", self.yT[lo:hi, 2:6, b * 128:(b + 1) * 128], pov, rdv, ALU.mult,
                        [ko, rd.key], [("yT", "a", b, kh)])
        yield
        self.cp("act", self.kTpad[:, l, 0:128], self.kTpad[:, l, T:T + 128], [("kTpad", l, "b")], [("kTpad", l, "h")])
        yield
        self.cp("act", self.vpad[:, l, 0, :], self.vpad[:, l, NB, :], [("vpad", l, "b", NB - 1)], [("vpad", l, "h")])

    def dn_pre(self, l):
        T = self.T
        pp = self.pp[:, l, :]
        for i in range(6):
            pad = self.pad.get()
            yield
            self.cp("act", pad.ap[:, 0:3], self.halo_dn[:, l, i, :], [("halo_dn", l, i)], [pad.key])
            pf, kf = self.inproj_F(l, 9 + i)
            yield
            self.cp("act", pad.ap[:, 3:3 + T], pf, [kf], [pad.key])
            cv = self.conv("dve", pad, pp[:, 48 + i * 4:52 + i * 4], None, l)
            yield
            self.cp("act", self.halo_dn[:, l, i, :], pad.ap[:, T:T + 3], [pad.key], [("halo_dn", l, i)])
            kind, c = divmod(i, 2)
            if kind == 2:
                yield
                sv = self.tmp.get()
                yield from self.sig3(sv.ap, cv.ap, [cv.key], [sv.key])
                yield
                self.tt("dve", self.vT_dn[:, c, :], cv.ap, sv.ap, ALU.mult, [cv.key, sv.key], [("vT_dn", c)])
                continue
            sv = self.tmp.get()
            yield from self.sig3(sv.ap, cv.ap, [cv.key], [sv.key])
            yield
            self.tt("dve", cv.ap, cv.ap, sv.ap, ALU.mult, [cv.key, sv.key], [cv.key])
            sqb = self.tmpb.get()
            yield
            self.act(sqb.ap, cv.ap, AF.Square, [cv.key], [sqb.key])
            pn, kn = self.psum()
            self.mm(pn, self.bones_b, sqb.ap, True, True, ["cstb", sqb.key], [kn])
            rn = self.tmp.get()
            yield
            self.act(rn.ap, pn, AF.Ln, [kn], [rn.key], bias=EPS)
            yield
            self.act(rn.ap, rn.ap, AF.Exp, [rn.key], [rn.key], scale=-0.5)
            if kind == 0:
                yield
                self.stt("dve", self.qT_dn[:, c, :], cv.ap, 0.125, rn.ap, ALU.mult, ALU.mult,
                         [cv.key, rn.key], [("qT_dn", c)])
            else:
                yield
                self.tt("dve", self.kT_dn[:, c, :], cv.ap, rn.ap, ALU.mult, [cv.key, rn.key], [("kT_dn", c)])

    def dn_scalars(self, l):
        NB = self.NB
        sc = self.sc
        BETA, X, AX_, EX, G, GC, NGC, EGC, CKBG, CDEC, SDEC, RX = range(12)
        self.BETA, self.G, self.GC, self.NGC, self.CKBG, self.CDEC, self.SDEC = BETA, G, GC, NGC, CKBG, CDEC, SDEC
        k = lambda i: ("sc", i)
        b_ = self.ba[:, :, 0:4]
        a_ = self.ba[:, :, 4:8]
        self.sig3_now(sc[:, BETA], b_, [("ba",)], [k(BETA)])
        dtb = self.rp[:, l, 12:16].unsqueeze(1).to_broadcast([128, NB, 4])
        self.tt("dve", sc[:, X], a_, dtb, ALU.add, [("ba",), "rp"], [k(X)])
        self.act(sc[:, AX_], sc[:, X], AF.Abs, [k(X)], [k(AX_)])
        self.act(sc[:, EX], sc[:, AX_], AF.Exp, [k(AX_)], [k(EX)], scale=-1.0)
        self.act(sc[:, EX], sc[:, EX], AF.Ln, [k(EX)], [k(EX)], bias=1.0)
        self.ts("dve", sc[:, RX], sc[:, X], 0.0, None, ALU.max, None, [k(X)], [k(RX)])
        self.tt("dve", sc[:, RX], sc[:, RX], sc[:, EX], ALU.add, [k(RX), k(EX)], [k(RX)])
        negA = self.der[:, l, 4:8].unsqueeze(1).to_broadcast([128, NB, 4])
        self.tt("dve", sc[:, G], sc[:, RX], negA, ALU.mult, [k(RX), ("der", l)], [k(G)])
        pc, kc = self.psum()
        for b in range(NB):
            self.mm(pc[:, b * 4:(b + 1) * 4], self.U_f, sc[:, G, b, :], True, True, ["cst", k(G)], [kc])
        self.cp("act", sc[:, GC].rearrange("p b h -> p (b h)"), pc[:, 0:4 * NB], [kc], [k(GC)])
        self.ts("dve", sc[:, NGC], sc[:, GC], -1.0, None, ALU.mult, None, [k(GC)], [k(NGC)])
        self.act(sc[:, EGC], sc[:, GC], AF.Exp, [k(GC)], [k(EGC)])
        self.tt("dve", sc[:, CKBG], sc[:, EGC], sc[:, BETA], ALU.mult, [k(EGC), k(BETA)], [k(CKBG)])

    def dn_block(self, l, b, bs, mid):
        tdt = self.tdt
        sc = self.sc
        k = lambda i: ("sc", i)
        blk = slice(b * 128, (b + 1) * 128)
        v4 = lambda ap: ap.rearrange("p (h j) -> p h j", h=4)
        ug = self.tmp.get()
        yield
        self.tt("dve", v4(ug.ap), self.U_f.unsqueeze(1).to_broadcast([128, 4, 128]),
                sc[:, self.G, b, :].unsqueeze(2).to_broadcast([128, 4, 128]), ALU.mult,
                ["cst", k(self.G)], [ug.key])
        pG, kG = self.psum()
        self.mm(pG, self.ones_f, ug.ap, True, True, ["cst", ug.key], [kG])
        pG4 = v4(pG)
        ngb = self.tmp.get()
        yield
        self.tt("dve", v4(ngb.ap), self.ones_f.unsqueeze(1).to_broadcast([128, 4, 128]),
                sc[:, self.NGC, b, :].unsqueeze(2).to_broadcast([128, 4, 128]), ALU.mult,
                ["cst", k(self.NGC)], [ngb.key])
        pD, kD = self.psum()
        self.mm(pD, self.ones_f, ug.ap, True, False, ["cst", ug.key], [kD])
        self.mm(pD, self.ident_f, ngb.ap, False, True, ["cst", ngb.key], [kD])
        yield
        self.act(bs["EG"], pG4, AF.Exp, [kG], [bs["EGk"]])
        r1 = self.tmp.get()
        r2 = self.tmp.get()
        yield
        self.act(r1.ap, pD, AF.Relu, [kD], [r1.key], scale=-1.0)
        yield
        self.act(r2.ap, pD, AF.Relu, [kD], [r2.key])
        yield
        self.act(r1.ap, r1.ap, AF.Exp, [r1.key], [r1.key], scale=-1.0)
        yield
        self.act(r2.ap, r2.ap, AF.Exp, [r2.key], [r2.key], scale=-1.0)
        yield
        self.tt("dve", sc[:, self.CDEC, b, :], pG4[:, :, 127], sc[:, self.GC, b, :], ALU.subtract,
                [kG, k(self.GC)], [("sc", self.CDEC, b)])
        yield
        self.act(sc[:, self.CDEC, b, :], sc[:, self.CDEC, b, :], AF.Exp, [("sc", self.CDEC, b)], [("sc", self.CDEC, b)])
        yield
        self.cp("dve", sc[:, self.SDEC, b, :], bs["EG"][:, :, 127], [bs["EGk"]], [("sc", self.SDEC, b)])
        if self.blk_stop <= 1:
            return
        pkt, kkt = self.psum()
        pkb = pkt.bitcast(BF16)
        for c in range(2):
            self.tr(pkb[:, c * 128:(c + 1) * 128], self.kT_dn[:, c, blk], self.ident_b, [("kT_dn", c), "cstb"], [kkt])
        for c in range(2):
            self.tr(pkb[:, 256 + c * 128:256 + (c + 1) * 128], self.vT_dn[:, c, blk], self.ident_b,
                    [("vT_dn", c), "cstb"], [kkt])
        h64 = lambda ap: ap.rearrange("p (h d) -> p h d", h=4)
        bc = lambda i: sc[:, i, b, :].unsqueeze(2).to_broadcast([128, 4, 64])
        import os
        var = os.environ.get("BLK_VAR", "abc")
        if "a" in var:
            yield
            self.tt("dve", h64(self.kbg[:, b, :]), h64(pkb[:, 0:256]), bc(self.CKBG), ALU.mult,
                    [kkt, k(self.CKBG)], [("kbg", b)])
        if "b" in var:
            yield
            self.tt("dve", h64(self.kdec[:, b, :]), h64(pkb[:, 0:256]), bc(self.CDEC), ALU.mult,
                    [kkt, ("sc", self.CDEC, b)], [("kdec", b)])
        if "c" in var:
            yield
            self.tt("dve", h64(self.vb[:, b, :]), h64(pkb[:, 256:512]), bc(self.BETA), ALU.mult,
                    [kkt, k(self.BETA)], [("vb", b)])
        if self.blk_stop <= 2:
            return
        pKK = [self.psum(), self.psum()]
        v22 = lambda ap: ap.rearrange("p (c hh j) -> p c hh j", c=2, hh=2)
        c2 = lambda ap, n: ap[:, 0:2 * n].rearrange("p (c j) -> p c j", c=2)
        for h in range(4):
            lo, c = (h % 2) * 64, h // 2
            self.mm(pKK[h % 2][0][:, c * 128:(c + 1) * 128], self.kT_dn[lo:lo + 64, c, blk],
                    self.kT_dn[lo:lo + 64, c, blk], True, True, [("kT_dn", c)], [pKK[h % 2][1]])
        yield
        self.tt("dve", v4(r2.ap), v4(r2.ap), self.SL_f.unsqueeze(1).to_broadcast([128, 4, 128]), ALU.mult,
                [r2.key, "cst"], [r2.key])
        for hh in range(2):
            yield
            self.tt("dve", v22(r2.ap)[:, :, hh, :], v22(r2.ap)[:, :, hh, :], c2(pKK[hh][0], 128), ALU.mult,
                    [r2.key, pKK[hh][1]], [r2.key])
        P = self.tmp.get()
        yield
        self.tt("dve", v4(P.ap), v4(r2.ap), sc[:, self.BETA, b, :].unsqueeze(2).to_broadcast([128, 4, 128]),
                ALU.mult, [r2.key, k(self.BETA)], [P.key])
        yield
        self.tt("dve", v4(r1.ap), v4(r1.ap), self.U_f.unsqueeze(1).to_broadcast([128, 4, 128]), ALU.mult,
                [r1.key, "cst"], [r1.key])
        pKQ = [self.psum(), self.psum()]
        for h in range(4):
            lo, c = (h % 2) * 64, h // 2
            self.mm(pKQ[h % 2][0][:, c * 128:(c + 1) * 128], self.kT_dn[lo:lo + 64, c, blk],
                    self.qT_dn[lo:lo + 64, c, blk], True, True, [("kT_dn", c), ("qT_dn", c)], [pKQ[h % 2][1]])
        at22 = bs["attnT"].rearrange("p (c hh) i -> p c hh i", c=2)
        for hh in range(2):
            yield
            self.tt("dve", at22[:, :, hh, :], v22(r1.ap)[:, :, hh, :], c2(pKQ[hh][0], 128), ALU.mult,
                    [r1.key, pKQ[hh][1]], [bs["attnTk"]])
        if self.blk_stop <= 3:
            return
        Pb = self.tmpb.get()
        yield
        self.cp("act", Pb.ap, P.ap, [P.key], [Pb.key])
        pPt, kPt = self.psum()
        for h in range(4):
            self.tr(v4(pPt)[:, h, :], v4(P.ap)[:, h, :], self.ident_f, [P.key, "cst"], [kPt])
        Ptb = self.tmpb.get()
        yield
        self.cp("act", Ptb.ap, pPt, [kPt], [Ptb.key])
        Tt = self.tmpb.get()
        yield
        self.tt("dve", v4(Tt.ap), self.ident_f.unsqueeze(1).to_broadcast([128, 4, 128]), v4(pPt), ALU.subtract,
                ["cst", kPt], [Tt.key])
        if self.blk_stop <= 4:
            return
        pM, kM = self.psum()
        pN, kN = self.psum()
        for h in range(4):
            self.mm(v4(pM)[:, h, :], v4(Ptb.ap)[:, h, :], v4(Pb.ap)[:, h, :], True, True, [Ptb.key, Pb.key], [kM])
        for h in range(4):
            self.mm(v4(pN)[:, h, :], v4(Pb.ap)[:, h, :], v4(Ptb.ap)[:, h, :], True, True, [Ptb.key, Pb.key], [kN])
        M = self.tmpb.get()
        N = self.tmpb.get()
        yield
        self.cp("act", M.ap, pM, [kM], [M.key])
        yield
        self.cp("act", N.ap, pN, [kN], [N.key])
        for r in range(1, 7):
            pX, kX = self.psum()
            for h in range(4):
                self.mm(v4(pX)[:, h, :], v4(M.ap)[:, h, :], v4(Tt.ap)[:, h, :], True, True, [M.key, Tt.key], [kX])
            M2 = N2 = None
            if r + 1 <= 6:
                pM, kM = self.psum()
                for h in range(4):
                    self.mm(v4(pM)[:, h, :], v4(N.ap)[:, h, :], v4(M.ap)[:, h, :], True, True, [M.key, N.key], [kM])
                M2 = self.tmpb.get()
                yield
                self.cp("act", M2.ap, pM, [kM], [M2.key])
            if r + 1 <= 5:
                pN, kN = self.psum()
                for h in range(4):
                    self.mm(v4(pN)[:, h, :], v4(M.ap)[:, h, :], v4(N.ap)[:, h, :], True, True, [M.key, N.key], [kN])
                N2 = self.tmpb.get()
                yield
                self.cp("act", N2.ap, pN, [kN], [N2.key])
            if r < 6:
                Tt2 = self.tmpb.get()
                yield
                self.tt("dve", Tt2.ap, Tt.ap, pX, ALU.add, [Tt.key, kX], [Tt2.key])
                Tt = Tt2
            else:
                yield
                self.tt("dve", bs["Ttb"].rearrange("p h i -> p (h i)"), Tt.ap, pX, ALU.add, [Tt.key, kX], [bs["Ttbk"]])
            M, N = M2, N2
        if self.blk_stop <= 5:
            return
        pu, ku = self.psum()
        for h in range(4):
            self.mm(pu[:, h * 64:(h + 1) * 64], bs["Ttb"][:, h, :], self.vb[:, b, h * 64:(h + 1) * 64], True, True,
                    [bs["Ttbk"], ("vb", b)], [ku])
        yield
        self.cp("act", bs["u_sb"], pu[:, 0:256], [ku], [bs["u_sbk"]])
        pw, kw = self.psum()
        for h in range(4):
            c = h // 2
            self.mm(v4(pw)[:, h, :], self.kbg[:, b, c * 128:(c + 1) * 128], bs["Ttb"][:, h, :], True, True,
                    [bs["Ttbk"], ("kbg", b)], [kw])
        pw22 = pw.rearrange("p (c hh i) -> p c hh i", c=2, hh=2)
        yield
        self.cp("act", bs["wT"][0:64], pw22[0:64, :, 0, :], [kw], [bs["wTk"] + (0,)])
        yield
        self.cp("act", bs["wT"][64:128], pw22[64:128, :, 1, :], [kw], [bs["wTk"] + (1,)])
        eg22 = bs["EG"].rearrange("p (c hh) i -> p c hh i", c=2)
        yield
        self.tt("dve", bs["qgT"][0:64], self.qT_dn[0:64, :, blk], eg22[0:64, :, 0, :], ALU.mult,
                [("qT_dn",), bs["EGk"]], [bs["qgTk"] + (0,)])
        yield
        self.tt("dve", bs["qgT"][64:128], self.qT_dn[64:128, :, blk], eg22[64:128, :, 1, :], ALU.mult,
                [("qT_dn",), bs["EGk"]], [bs["qgTk"] + (1,)])
        if self.blk_stop <= 6:
            return
        if b > 0:
            yield ("wait", ("seq", mid, b - 1))
        S = self.S[:, l]
        Sb = self.Sb[:, l]
        pws = [self.psum(), self.psum()]
        for h in range(4):
            lo, c = (h % 2) * 64, h // 2
            self.mm(pws[h % 2][0][:, c * 64:(c + 1) * 64], bs["wT"][lo:lo + 64, c, :], Sb[lo:lo + 64, c, :], True, True,
                    [bs["wTk"], ("Sb", l)], [pws[h % 2][1]])
        vn = self.tmpb.get()
        vnew = vn.ap[:, 0:256]
        q64 = lambda ap: ap.rearrange("p (c hh d) -> p c hh d", c=2, hh=2)
        for hh in range(2):
            yield
            self.tt("dve", q64(vnew)[:, :, hh, :], q64(bs["u_sb"])[:, :, hh, :], c2(pws[hh][0], 64), ALU.subtract,
                    [bs["u_sbk"], pws[hh][1]], [vn.key])
        po = [self.psum(), self.psum()]
        for h in range(4):
            lo, c = (h % 2) * 64, h // 2
            self.mm(po[h % 2][0][:, c * 64:(c + 1) * 64], bs["qgT"][lo:lo + 64, c, :], Sb[lo:lo + 64, c, :], True, False,
                    [bs["qgTk"], ("Sb", l)], [po[h % 2][1]])
            self.mm(po[h % 2][0][:, c * 64:(c + 1) * 64], bs["attnT"][:, h, :], vnew[:, h * 64:(h + 1) * 64], False, True,
                    [bs["attnTk"], vn.key], [po[h % 2][1]])
        pst, kst = self.psum()
        for h in range(4):
            c = h // 2
            self.mm(pst[:, h * 64:(h + 1) * 64], self.kdec[:, b, c * 128:(c + 1) * 128], vnew[:, h * 64:(h + 1) * 64],
                    True, True, [("kdec", b), vn.key], [kst])
        for h in range(4):
            lo, c = (h % 2) * 64, h // 2
            yield
            self.stt("dve", S[lo:lo + 64, c, :], S[lo:lo + 64, c, :], sc[lo:lo + 64, self.SDEC, b, h:h + 1],
                     pst[lo:lo + 64, h * 64:(h + 1) * 64], ALU.mult, ALU.add,
                     [("S", l), ("sc", self.SDEC, b), kst], [("S", l)])
        yield
        self.cp("act", Sb, S, [("S", l)], [("Sb", l)])
        yield ("set", ("seq", mid, b))
        if self.blk_stop <= 7:
            return
        o = self.tmp.get()
        ov = o.ap[:, 0:256]
        for hh in range(2):
            yield
            self.cp("act", q64(ov)[:, :, hh, :], c2(po[hh][0], 64), [po[hh][1]], [o.key])
        sq = self.tmp.get()
        sqv = sq.ap[:, 0:256]
        yield
        self.tt("dve", sqv, ov, ov, ALU.mult, [o.key], [sq.key])
        ss4 = sq.ap[:, 256:260]
        yield
        self.s.add("dve", lambda e: e.tensor_reduce(out=ss4, in_=sqv.rearrange("p (h d) -> p h d", h=4),
                                                    axis=AX.X, op=ALU.add), [sq.key], [sq.key])
        yield
        self.act(ss4, ss4, AF.Ln, [sq.key], [sq.key], bias=EPS, scale=1.0 / 64)
        yield
        self.act(ss4, ss4, AF.Exp, [sq.key], [sq.key], scale=-0.5)
        yield
        self.tt("dve", h64(ov), h64(ov), ss4.unsqueeze(2).to_broadcast([128, 4, 64]), ALU.mult,
                [o.key, sq.key], [o.key])
        yd = self.tmpb.get()
        yield
        self.tt("dve", yd.ap[:, 0:256], ov, self.gz[:, b, :], ALU.mult, [o.key, ("gz",)], [yd.key])
        pT_, kT_ = self.psum()
        pTb = pT_.bitcast(BF16)
        for c in range(2):
            self.tr(pTb[:, c * 128:(c + 1) * 128], yd.ap[:, c * 128:(c + 1) * 128], self.ident_b, [yd.key, "cstb"], [kT_])
        yield
        self.cp("act", self.yT[:, 6:8, blk], pTb[:, 0:256].rearrange("p (c i) -> p c i", c=2), [kT_], [("yT", "d", b)])

    def outproj(self, l):
        NB = self.NB
        for dq in range(4):
            w = self.wbig.get()
            self.dma("pool", w.ap, self.wout_d[l].rearrange("p (k n) -> p k n", k=8)[:, :, dq * 256:(dq + 1) * 256],
                     [], [w.key])
            for m in range(NB):
                po, ko = self.psum()
                for c in range(8):
                    self.mm(po[:, 0:256], self.yT[:, c, m * 128:(m + 1) * 128], w.ap[:, c, :], c == 0, c == 7,
                            [("yT",), w.key], [ko])
                hs = self.h[:, m, dq * 256:(dq + 1) * 256]
                self.tt("dve", hs, hs, po[:, 0:256], ALU.add, [ko, ("h", m, dq // 2, dq % 2)], [("h", m, dq // 2, dq % 2)])

    def mixer(self, l, gtile):
        self.norm_T(l, 1)
        mx = self.mix

        self.mix_id += 1
        mid = self.mix_id

        def sx():
            if 'lru' in mx:
                for c in range(2):
                    yield from self.lru(l, c)
            if 'att' in mx:
                yield ("wait", ("tm", mid))
                yield from self.attention(l, gtile)

        def sy():
            if 'tm' in mx:
                self.inproj_TM(l)
            yield ("set", ("tm", mid))
            if 'sc' in mx:
                self.dn_scalars(l)
            yield
            if 'pre' in mx:
                yield from self.dn_pre(l)
            yield ("set", ("pre", mid))
            if 'blk' in mx:
                for b in range(0, self.NB, 2):
                    yield from self.dn_block(l, b, self.bsY, mid)

        def sz():
            yield ("wait", ("pre", mid))
            if 'blk' in mx:
                for b in range(1, self.NB, 2):
                    yield from self.dn_block(l, b, self.bsZ, mid)

        if self.ilv:
            self.interleave([(self.cxX, sx()), (self.cxY, sy()), (self.cxZ, sz())], weights=self.ilv_w)
        else:
            self.interleave([(self.cxX, sx())])
            self.interleave([(self.cxY, sy()), (self.cxZ, sz())])
        if 'out' in mx:
            self.outproj(l)

    def ple(self, l, seq, t0):
        NB, T = self.NB, self.T
        self.norm_T(l, 3)
        self.dma("pool", self.pbf, self.p_d[l, seq, t0:t0 + T, :].rearrange("(m p) n -> p m n", p=128), [], ["pbf"])
        self.dma("pool", self.wpp, self.wpp_d[l].rearrange("p (k n) -> p k n", k=2), [], ["wpp"])
        for m in range(NB):
            pt, kt = self.psum()
            pb = pt.bitcast(BF16)
            for c in range(2):
                self.tr(pb[:, c * 128:(c + 1) * 128], self.pbf[:, m, c * 128:(c + 1) * 128], self.ident_b,
                        ["pbf", "cstb"], [kt])
            self.cp("act", self.pT[:, :, m * 128:(m + 1) * 128], pb[:, 0:256].rearrange("p (c i) -> p c i", c=2),
                    [kt], [("pT", m)])
        for dq in range(4):
            w = self.wbig.get()
            self.dma("pool", w.ap, self.wpg_d[l].rearrange("p (k n) -> p k n", k=8)[:, :, dq * 256:(dq + 1) * 256],
                     [], [w.key])
            for m in range(NB):
                pg, kg = self.psum()
                for k in range(8):
                    self.mm(pg[:, 0:256], self.xnT[:, k, m * 128:(m + 1) * 128], w.ap[:, k, :], k == 0, k == 7,
                            [("xnT", m), w.key], [kg])
                pj, kj = self.psum()
                for c in range(2):
                    self.mm(pj[:, 0:256], self.pT[:, c, m * 128:(m + 1) * 128], self.wpp[:, c, dq * 256:(dq + 1) * 256],
                            c == 0, c == 1, [("pT", m), "wpp"], [kj])
                sg = self.tmp.get()
                sgv = sg.ap[:, 0:256]
                self.act(sgv, pg[:, 0:256], AF.Sigmoid, [kg], [sg.key])
                self.tt("dve", sgv, sgv, pj[:, 0:256], ALU.mult, [sg.key, kj], [sg.key])
                hs = self.h[:, m, dq * 256:(dq + 1) * 256]
                hk = ("h", m, dq // 2, dq % 2)
                self.tt("dve", hs, hs, sgv, ALU.add, [sg.key, hk], [hk])

    def final(self, seq, t0):
        NB, T = self.NB, self.T
        ss = self.small[:, 0:NB]
        for m in range(NB):
            self.act(self.junk, self.h[:, m, :], AF.Square, [("h", m)], [("small", "ss", m), "junk"],
                     accum_out=self.small[:, m:m + 1])
        self.act(ss, ss, AF.Ln, [("small", "ss")], [("small", "ss")], bias=EPS, scale=1.0 / D)
        self.act(ss, ss, AF.Exp, [("small", "ss")], [("small", "ss")], scale=-0.5)
        for m in range(NB):
            self.stt("dve", self.h[:, m, :], self.h[:, m, :], self.small[:, m:m + 1], self.fing, ALU.mult, ALU.mult,
                     [("h", m), ("small", "ss"), "fing"], [("h", m)])
        for m in range(NB):
            op = self.dma("sp", self.out_d[seq, t0 + m * 128:t0 + (m + 1) * 128, :], self.h[:, m, :], [("h", m)], [])
            self.out_ops.append(op)

    def build(self):
        self.out_ops = []
        self.setup()
        for seq in range(self.nseq):
            self.zero_state()
            for t in range(self.NT):
                t0 = t * self.T
                for m in range(self.NB):
                    self.dma("sp", self.h[:, m, :], self.x_d[seq, t0 + m * 128:t0 + (m + 1) * 128, :], [], [("h", m)])
                for l in range(self.L):
                    if 'f1' in self.phases:
                        self.norm_T(l, 0)
                        self.ffn(l, 0)
                    if 'mix' in self.phases:
                        self.mixer(l, t)
                    if 'f2' in self.phases:
                        self.norm_T(l, 2)
                        self.ffn(l, 1)
                    if 'ple' in self.phases:
                        self.ple(l, seq, t0)
                self.final(seq, t0)
        self.s.analyze()
        self.s.emit(self.out_ops + self.dump_ops)
        return self.nc


_SHARED_CACHE = {}


def kernel(**inputs):
    inp = {k: np.asarray(v) for k, v in inputs.items()}
    shared = prep_shared(inp)
    n = 8
    b = Builder(nseq=2, ntiles=SEQ // 512, nlayers=DEPTH, T=512)
    nc = b.build()
    in_maps = []
    for c in range(n):
        m = dict(shared)
        m["x"] = np.ascontiguousarray(inp["x"][2 * c:2 * c + 2])
        m["p"] = np.ascontiguousarray(inp["p"][:, 2 * c:2 * c + 2])
        in_maps.append(m)
    res = run_bass_kernel_spmd(nc, in_maps, core_ids=list(range(n)))
    return np.concatenate([r["out"] for r in res.results], axis=0).astype(np.float32)
```
